# Optimizing a Trainium2 kernel written in Bass

```python
import jax, jax.numpy as jnp
from jax import lax
import numpy as np

D_MODEL = 1024
BATCH = 16
SEQ = 2048
DEPTH = 4

GRID_W = 64
CTX_LEN = 256
N_MIXERS = 2
EXPAND = 2
E_WIDTH = EXPAND * D_MODEL
HEAD_K = 128
N_HEADS = E_WIDTH // HEAD_K
HEAD_V = E_WIDTH // N_HEADS
CHUNK = 16
CONV_W = 31
EPS = 1e-6

kernel_name = "hgrn2_conformer_hybrid_dit"


def _rmsnorm(x, g):
    xf = x.astype(jnp.float32)
    y = xf * lax.rsqrt(jnp.mean(xf * xf, axis=-1, keepdims=True) + EPS)
    return (y * g.astype(jnp.float32)).astype(x.dtype)


def _layernorm(x, g, b):
    xf = x.astype(jnp.float32)
    xc = xf - jnp.mean(xf, axis=-1, keepdims=True)
    var = jnp.mean(xc * xc, axis=-1, keepdims=True)
    return (xc * lax.rsqrt(var + EPS) * g.astype(jnp.float32) + b.astype(jnp.float32)).astype(x.dtype)


def _ada(cvec, w, b):
    m = jax.nn.silu(cvec) @ w + b
    return jnp.split(m, 3, axis=-1)


def _modulate(x, g, shift, scale):
    return _rmsnorm(x, g) * (1 + scale) + shift


def _hgrn_lower_bounds(lb_logits):
    p = jax.nn.softmax(lb_logits.astype(jnp.float32), axis=1)
    return jnp.cumsum(p, axis=1) - p[:, :1]


def _heads(t):
    return t.reshape(t.shape[0], t.shape[1], N_HEADS, -1).astype(jnp.float32)


def _forget(z, lb):
    f = lb + (1 - lb) * jax.nn.sigmoid(z.astype(jnp.float32))
    return _heads(jnp.log(f)), _heads(1 - f)


def _chunk_gla(q, k, v, g, s0):
    b_, length, h_, _ = q.shape
    n = length // CHUNK

    def to_chunks(t):
        return t.reshape(b_, n, CHUNK, h_, t.shape[-1]).transpose(1, 0, 3, 2, 4)

    mask = jnp.tril(jnp.ones((CHUNK, CHUNK), dtype=bool))
    mid = CHUNK // 2

    def step(state, inp):
        qc, kc, vc, gc = inp
        bcum = jnp.cumsum(gc, axis=-2)
        b_ref = bcum[..., mid:mid + 1, :]
        b_end = bcum[..., -1:, :]
        inter = jnp.einsum('bhck,bhkv->bhcv', qc * jnp.exp(bcum), state)
        scores = jnp.einsum('bhtk,bhsk->bhts', qc * jnp.exp(bcum - b_ref), kc * jnp.exp(b_ref - bcum))
        intra = jnp.einsum('bhts,bhsv->bhtv', jnp.where(mask, scores, 0.0), vc)
        new_state = (jnp.exp(b_end[..., 0, :])[..., None] * state
                     + jnp.einsum('bhsk,bhsv->bhkv', kc * jnp.exp(b_end - bcum), vc))
        return new_state, inter + intra

    s_final, o = lax.scan(step, s0, (to_chunks(q), to_chunks(k), to_chunks(v), to_chunks(g)))
    o = o.transpose(1, 0, 3, 2, 4).reshape(b_, length, h_, v.shape[-1])
    return o, s_final


def _chunk_gla_rev(q, k, v, g, s0):
    flip = lambda t: jnp.flip(t, axis=1)
    o, s_final = _chunk_gla(flip(q), flip(k), flip(v), flip(g), s0)
    return flip(o), s_final


def _hgrn_project(h, w_in, lb_f, lb_b):
    p = h @ w_in
    q, v, zf, zb, gate = jnp.split(p, 5, axis=-1)
    gf, kf = _forget(zf, lb_f)
    gb, kb = _forget(zb, lb_b)
    return _heads(jax.nn.silu(q)), _heads(v), gf, kf, gb, kb, gate


def _hgrn_readout(o, gate, norm_g, w_out):
    o = _rmsnorm(o, norm_g)
    o = o.reshape(o.shape[0], o.shape[1], E_WIDTH).astype(gate.dtype)
    return (o * jax.nn.silu(gate)) @ w_out


def _hgrn2_mixer(h, hc, w_in, lb_f, lb_b, norm_g, w_out, need_ctx):
    q, v, gf, kf, gb, kb, gate = _hgrn_project(h, w_in, lb_f, lb_b)
    qc, vc, gfc, kfc, gbc, kbc, gatec = _hgrn_project(hc, w_in, lb_f, lb_b)
    s0 = jnp.zeros((hc.shape[0], N_HEADS, HEAD_K, HEAD_V), jnp.float32)
    oc_f, sc_f = _chunk_gla(qc, kfc, vc, gfc, s0)
    oc_b, sc_b = _chunk_gla_rev(qc, kbc, vc, gbc, s0)
    o_f, _ = _chunk_gla(q, kf, v, gf, sc_f)
    o_b, _ = _chunk_gla_rev(q, kb, v, gb, sc_b)
    y = _hgrn_readout(o_f + o_b, gate, norm_g, w_out)
    yc = _hgrn_readout(oc_f + oc_b, gatec, norm_g, w_out) if need_ctx else None
    return y, yc


def _dwconv(u, w, b):
    pad = CONV_W // 2
    out = lax.conv_general_dilated(u, w[:, None, :], window_strides=(1,), padding=[(pad, pad)],
                                   dimension_numbers=('NWC', 'WIO', 'NWC'),
                                   feature_group_count=u.shape[-1])
    return out + b


def _latent_dwconv(u, w, b, vertical):
    b_, length, ch = u.shape
    rows = length // GRID_W
    grid = u.reshape(b_, rows, GRID_W, ch)
    if vertical:
        grid = grid.transpose(0, 2, 1, 3)
    n1, n2 = grid.shape[1], grid.shape[2]
    out = _dwconv(grid.reshape(b_ * n1, n2, ch), w, b).reshape(b_, n1, n2, ch)
    if vertical:
        out = out.transpose(0, 2, 1, 3)
    return out.reshape(b_, length, ch)


def _conv_mixer(h, w_in, b_in, dw, dw_b, ln_g, ln_b, w_out, b_out, conv_fn):
    p = h @ w_in + b_in
    a, gl, gate = jnp.split(p, 3, axis=-1)
    u = a * jax.nn.sigmoid(gl)
    u = conv_fn(u, dw, dw_b)
    u = jax.nn.silu(_layernorm(u, ln_g, ln_b))
    return (u * jax.nn.silu(gate)) @ w_out + b_out


def setup_inputs(seed: int = 0) -> dict:
    key = jax.random.key(seed)
    ks = jax.random.split(key, 20)
    n_a = (DEPTH + 1) // 2
    n_b = DEPTH // 2
    f32 = jnp.float32

    def w(k, shape, fan_in):
        return jax.random.normal(k, shape, f32) * fan_in ** -0.5

    def small(k, shape, s=0.02):
        return jax.random.normal(k, shape, f32) * s

    return {
        "x": jax.random.normal(ks[0], (BATCH, SEQ, D_MODEL), f32),
        "c": jax.random.normal(ks[1], (BATCH, D_MODEL), f32),
        "ctx": jax.random.normal(ks[2], (BATCH, CTX_LEN, D_MODEL), f32),
        "c_ctx": jax.random.normal(ks[3], (D_MODEL,), f32),
        "norm_g": 1.0 + small(ks[4], (DEPTH, D_MODEL)),
        "ada_w": w(ks[5], (DEPTH, D_MODEL, 3 * D_MODEL), D_MODEL),
        "ada_b": small(ks[6], (DEPTH, 3 * D_MODEL)),
        "hgrn_w_in": w(ks[7], (n_a, D_MODEL, 5 * E_WIDTH), D_MODEL),
        "hgrn_lb_logits": small(ks[8], (2, n_a, E_WIDTH), 0.1),
        "hgrn_norm_g": 1.0 + small(ks[9], (n_a, HEAD_V)),
        "hgrn_w_out": w(ks[10], (n_a, E_WIDTH, D_MODEL), E_WIDTH),
        "conv_w_in": w(ks[11], (n_b, D_MODEL, 3 * E_WIDTH), D_MODEL),
        "conv_b_in": small(ks[12], (n_b, 3 * E_WIDTH)),
        "conv_dw": w(ks[13], (n_b, CONV_W, E_WIDTH), CONV_W),
        "conv_dw_b": small(ks[14], (n_b, E_WIDTH)),
        "conv_ln_g": 1.0 + small(ks[15], (n_b, E_WIDTH)),
        "conv_ln_b": small(ks[16], (n_b, E_WIDTH)),
        "conv_w_out": w(ks[17], (n_b, E_WIDTH, D_MODEL), E_WIDTH),
        "conv_b_out": small(ks[18], (n_b, D_MODEL)),
        "final_norm_g": 1.0 + small(ks[19], (D_MODEL,)),
    }


def reference(x, c, ctx, c_ctx, norm_g, ada_w, ada_b, hgrn_w_in, hgrn_lb_logits, hgrn_norm_g, hgrn_w_out,
              conv_w_in, conv_b_in, conv_dw, conv_dw_b, conv_ln_g, conv_ln_b, conv_w_out, conv_b_out,
              final_norm_g):
    lb = _hgrn_lower_bounds(hgrn_lb_logits)
    for i in range(DEPTH):
        mixer = i % N_MIXERS
        j = i // N_MIXERS
        need_ctx = any(m % N_MIXERS == 0 for m in range(i + 1, DEPTH))
        shift, scale, gate = _ada(c, ada_w[i], ada_b[i])
        h = _modulate(x, norm_g[i], shift[:, None], scale[:, None])
        if mixer == 0 or need_ctx:
            shift_c, scale_c, gate_c = _ada(c_ctx, ada_w[i], ada_b[i])
            hc = _modulate(ctx, norm_g[i], shift_c, scale_c)
        if mixer == 0:
            y, yc = _hgrn2_mixer(h, hc, hgrn_w_in[j], lb[0, j], lb[1, j], hgrn_norm_g[j], hgrn_w_out[j], need_ctx)
        else:
            vertical = (j % 2 == 1)
            lat_conv = lambda u, dw, dwb: _latent_dwconv(u, dw, dwb, vertical)
            y = _conv_mixer(h, conv_w_in[j], conv_b_in[j], conv_dw[j], conv_dw_b[j], conv_ln_g[j], conv_ln_b[j],
                            conv_w_out[j], conv_b_out[j], lat_conv)
            yc = (_conv_mixer(hc, conv_w_in[j], conv_b_in[j], conv_dw[j], conv_dw_b[j], conv_ln_g[j], conv_ln_b[j],
                              conv_w_out[j], conv_b_out[j], _dwconv) if need_ctx else None)
        x = x + gate[:, None] * y
        if need_ctx:
            ctx = ctx + gate_c * yc
    return _rmsnorm(x, final_norm_g)
```

```python
from contextlib import ExitStack
import numpy as np
import concourse.bass as bass
import concourse.mybir as mybir
from concourse.bass_utils import run_bass_kernel_spmd

F32 = mybir.dt.float32
BF16 = mybir.dt.bfloat16
AF = mybir.ActivationFunctionType
ALU = mybir.AluOpType

P = 128
D = 1024
KC = 8
E = 2048
EC = 16
NT = 4608
NLAT = 4096
EPS = 1e-6
SKIP = set()
ENGS = ("pe", "act", "dve", "pool", "sp")


class Buf:
    __slots__ = ("name", "w", "r", "sem", "val")

    def __init__(self, name):
        self.name = name
        self.w = None
        self.r = []
        self.sem = None
        self.val = 0


class Op:
    __slots__ = ("eng", "fn", "deps", "sig", "seq", "dma", "chan", "val")


class Prog:
    def __init__(self, nc):
        self.nc = nc
        self.ops = {e: [] for e in ENGS}
        self.bufs = []
        self.all = []

    def buf(self, name):
        b = Buf(name)
        self.bufs.append(b)
        return b

    def bufs_n(self, name, n):
        return [self.buf(f"{name}{i}") for i in range(n)]

    def op(self, eng, fn, reads=(), writes=(), chan=None):
        o = Op()
        o.eng = eng
        o.fn = fn
        o.sig = False
        o.seq = 0
        o.dma = chan is not None
        o.chan = chan
        o.val = 0
        deps = []
        for b in reads:
            if b.w is not None:
                deps.append((b.w, "raw"))
        for b in writes:
            if b.w is not None:
                deps.append((b.w, "waw"))
            for r in b.r:
                deps.append((r, "war"))
        dd = []
        seen = set()
        for (d, kind) in deps:
            if d is o or id(d) in seen:
                continue
            if (not d.dma) and d.eng == eng:
                if eng == "pe":
                    continue
                if kind != "raw":
                    continue
            seen.add(id(d))
            dd.append(d)
        o.deps = dd
        for b in reads:
            b.r.append(o)
        for b in writes:
            b.w = o
            b.r = []
        self.ops[eng].append(o)
        self.all.append(o)
        return o

    def flush(self, es):
        nc = self.nc
        if not hasattr(self, "cpool"):
            self.cpool = [nc.alloc_semaphore(f"c_{i}") for i in range(26)]
            self.cval = [0] * 26
            self.nflush = 0
        self.nflush += 1
        nf = self.nflush
        sems = {e: nc.alloc_semaphore(f"s_{e}_{nf}") for e in ENGS if e != "sp"}
        for o in self.all:
            for d in o.deps:
                d.sig = True
        chans = []
        for o in self.all:
            if o.dma and o.chan.sem is None:
                i = len(chans)
                o.chan.sem = self.cpool[i]
                o.chan.val = self.cval[i]
                chans.append(o.chan)
        for e in ENGS:
            if e == "sp":
                continue
            for o in reversed(self.ops[e]):
                if not o.dma:
                    o.sig = True
                    break
        cnt = {e: 0 for e in ENGS}
        for e in ENGS:
            for o in self.ops[e]:
                if o.dma:
                    o.chan.val += 16
                    o.val = o.chan.val
                elif o.sig:
                    cnt[e] += 1
                    o.seq = cnt[e]
        final = dict(cnt)
        for i, c in enumerate(chans):
            self.cval[i] = c.val
        block = es.enter_context(nc.Block())

        def run(e, eng):
            waited = {}
            for o in self.ops[e]:
                need = {}
                for d in o.deps:
                    if d.dma:
                        key = ("c", id(d.chan))
                        sem = d.chan.sem
                        v = d.val
                    else:
                        key = ("e", d.eng)
                        sem = sems[d.eng]
                        v = d.seq
                    if waited.get(key, 0) >= v:
                        continue
                    if key not in need or need[key][1] < v:
                        need[key] = (sem, v)
                for key, (sem, v) in need.items():
                    eng.wait_ge(sem, v)
                    waited[key] = v
                ins = o.fn(eng)
                if o.dma:
                    ins.then_inc(o.chan.sem, 16)
                elif o.sig:
                    ins.then_inc(sems[e], 1)
            for f in ENGS:
                if f == "sp":
                    continue
                if final[f] > 0 and waited.get(("e", f), 0) < final[f]:
                    eng.wait_ge(sems[f], final[f])
            for c in chans:
                if waited.get(("c", id(c)), 0) < c.val:
                    eng.wait_ge(c.sem, c.val)

        block.tensor(lambda eng: run("pe", eng))
        block.scalar(lambda eng: run("act", eng))
        block.vector(lambda eng: run("dve", eng))
        block.gpsimd(lambda eng: run("pool", eng))
        block.sync(lambda eng: run("sp", eng))
        for b in self.bufs:
            b.w = None
            b.r = []
            b.sem = None
            b.val = 0
        self.ops = {e: [] for e in ENGS}
        self.all = []


def bc(ap, shape):
    return ap.broadcast_to(list(shape))


def build(n_layers=4, dbg=False):
    nc = bass.Bass("TRN2", target_bir_lowering=False)
    pg = Prog(nc)

    def din(name, shape, dt=F32):
        return nc.dram_tensor(name, list(shape), dt, kind="ExternalInput").ap()

    xT0 = din("xT0", [D, NT])
    cT_d = din("cT", [P, KC * 3])
    normg_d = din("normgT", [P, 4 * KC])
    fing_d = din("fingT", [P, KC])
    adaw_d = din("ada_w", [4, D, 3 * D])
    adab_d = din("adabT", [P, 4 * 24])
    hwin_d = din("hgrn_w_in", [2, D, 5 * E])
    hwout_d = din("hgrn_w_out", [2, E, D])
    lb_d = din("lbT", [P, 2 * 2 * 16])
    hng_d = din("hngT", [P, 2])
    cwin_d = din("conv_w_in", [2, D, 3 * E])
    cwout_d = din("conv_w_out", [2, E, D])
    cbin_d = din("cbinT", [P, 2 * 48])
    cdw_d = din("cdwT", [P, 2 * 16 * 31])
    cdwb_d = din("cdwbT", [P, 2 * 16])
    clng_d = din("clngT", [P, 2 * 16])
    clnb_d = din("clnbT", [P, 2 * 16])
    cbout_d = din("cboutT", [P, 2 * 8])
    ident_d = din("ident", [P, P])
    mask_d = din("masks", [P, 2 * P])
    reset_d = din("resets", [P, 2 * 512])
    outT = nc.dram_tensor("outT", [D, NLAT], F32, kind="ExternalOutput").ap()

    knd = "ExternalOutput" if dbg else "Internal"
    XT = [xT0] + [nc.dram_tensor(f"xT{i}", [D, NT], F32, kind=knd).ap() for i in range(1, 5)]
    OG = nc.dram_tensor("og", [E, NT], BF16, kind=knd).ap()
    UC = nc.dram_tensor("uc", [E, NT], F32, kind=knd).ap()
    HTd = nc.dram_tensor("hTd", [D, NT], BF16, kind=knd).ap()
    XTb = [[pg.buf(f"xt{i}_{t}") for t in range(9)] for i in range(5)]
    OGb = [[pg.buf(f"og{h}_{t}") for t in range(9)] for h in range(16)]
    UCb = [[pg.buf(f"uc{c}_{t}") for t in range(9)] for c in range(16)]
    HTdb = [pg.buf(f"htd{t}") for t in range(9)]

    top = ExitStack()

    uniq = [0]

    def sb(es, name, shape, dt=F32):
        uniq[0] += 1
        return es.enter_context(nc.sbuf_tensor(f"{name}_{uniq[0]}", list(shape), dt))

    def ps(es, name, shape, dt=F32):
        uniq[0] += 1
        return es.enter_context(nc.psum_tensor(f"{name}_{uniq[0]}", list(shape), dt))

    hT = sb(top, "hT", [P, KC, NT], BF16)
    hTb = [pg.buf(f"hT{t}") for t in range(9)]
    MOD = sb(top, "MOD", [P, 4, 3, KC, 3])
    MODb = pg.buf("MOD")
    BG = sb(top, "BG", [P, 4, KC, 3])
    LBA = sb(top, "LBA", [P, 2, 2, 16])
    LBB = sb(top, "LBB", [P, 2, 2, 16])
    LBN = sb(top, "LBN", [P, 2, 2, 16])
    normg = sb(top, "normg", [P, 4, KC])
    fing = sb(top, "fing", [P, KC])
    hng = sb(top, "hng", [P, 2])
    cbin = sb(top, "cbin", [P, 2, 48])
    cdw = sb(top, "cdw", [P, 2, 16, 31])
    cdwb = sb(top, "cdwb", [P, 2, 16])
    clng = sb(top, "clng", [P, 2, 16])
    clnb = sb(top, "clnb", [P, 2, 16])
    cbout = sb(top, "cbout", [P, 2, 8])
    ident = sb(top, "identf", [P, P])
    identb = sb(top, "identb", [P, P], BF16)
    onesb = sb(top, "onesb", [P, P], BF16)
    onesf = sb(top, "onesf", [P, P])
    masks = sb(top, "masksb", [P, 2, P])
    resets = sb(top, "resetsb", [P, 2, 512])
    zcol = sb(top, "zcol", [P, 1])
    CONSTb = pg.buf("consts")

    with ExitStack() as es:
        cT = sb(es, "cTs", [P, KC, 3])
        th = sb(es, "cth", [P, KC, 3])
        scT = sb(es, "scT", [P, KC, 3])
        adab = sb(es, "adab", [P, 4, 24])
        lbl = sb(es, "lbl", [P, 2, 2, 16])
        lbt = sb(es, "lbt", [P, 2, 16])
        adaT = sb(es, "adaT", [P, 4, 24, 3])
        aw = [sb(es, f"aw{i}", [P, KC, 768]) for i in range(2)]
        awb = pg.bufs_n("aw", 2)
        pada = [ps(es, f"pada{i}", [P, 512]) for i in range(4)]
        padab = pg.bufs_n("pada", 4)
        smallb = pg.buf("small")
        scb = pg.buf("scT")
        adaTb = pg.buf("adaT")

        loads = [(cT[:].rearrange("p a b -> p (a b)"), cT_d), (normg[:].rearrange("p a b -> p (a b)"), normg_d),
                 (fing[:], fing_d), (adab[:].rearrange("p a b -> p (a b)"), adab_d),
                 (lbl[:].rearrange("p a b c -> p (a b c)"), lb_d), (hng[:], hng_d),
                 (cbin[:].rearrange("p a b -> p (a b)"), cbin_d), (cdw[:].rearrange("p a b c -> p (a b c)"), cdw_d),
                 (cdwb[:].rearrange("p a b -> p (a b)"), cdwb_d), (clng[:].rearrange("p a b -> p (a b)"), clng_d),
                 (clnb[:].rearrange("p a b -> p (a b)"), clnb_d), (cbout[:].rearrange("p a b -> p (a b)"), cbout_d),
                 (ident[:], ident_d), (masks[:].rearrange("p a b -> p (a b)"), mask_d),
                 (resets[:].rearrange("p a b -> p (a b)"), reset_d)]
        for i, (dst, src) in enumerate(loads):
            b = pg.buf(f"ld{i}")
            pg.op("sp", lambda e, dst=dst, src=src: e.dma_start(out=dst, in_=src[:, :]), writes=[b], chan=b)

        def consts(e):
            e.tensor_copy(out=identb[:], in_=ident[:])
            e.memset(onesb[:], 1.0)
            e.memset(onesf[:], 1.0)
            e.memset(zcol[:], 0.0)
            return e.memset(BG[:].rearrange("p a b c -> p (a b c)"), 0.0)
        all_ld = [b for b in pg.bufs if b.name.startswith("ld")]
        pg.op("dve", consts, reads=all_ld, writes=[CONSTb])
        pg.op("act", lambda e: e.activation(out=th[:], in_=cT[:], func=AF.Tanh, scale=0.5), reads=all_ld, writes=[scb])

        pg.op("dve", lambda e: e.tensor_scalar(out=th[:], in0=th[:], scalar1=0.5, scalar2=0.5, op0=ALU.mult, op1=ALU.add), reads=[scb, CONSTb], writes=[scb])
        pg.op("dve", lambda e: e.tensor_tensor(out=scT[:], in0=th[:], in1=cT[:], op=ALU.mult), reads=[scb], writes=[scb])
        def lb1(e):
            return e.tensor_tensor(out=lbt[:], in0=lbl[:, :, 1, :], in1=lbl[:, :, 0, :], op=ALU.subtract)
        lbb = pg.buf("lbb")
        pg.op("dve", lb1, reads=all_ld + [scb], writes=[lbb])
        pg.op("act", lambda e: e.activation(out=lbt[:], in_=lbt[:], func=AF.Tanh, scale=0.5), reads=[lbb], writes=[lbb])

        def lb2(e):
            e.memset(LBA[:, 0, :, :], 0.5)
            e.memset(LBB[:, 0, :, :], 0.5)
            e.memset(LBN[:, 0, :, :], -0.5)
            e.tensor_scalar(out=LBA[:, 1, :, :], in0=lbt[:], scalar1=0.25, scalar2=0.75, op0=ALU.mult, op1=ALU.add)
            e.tensor_scalar(out=LBB[:, 1, :, :], in0=lbt[:], scalar1=-0.25, scalar2=0.25, op0=ALU.mult, op1=ALU.add)
            return e.tensor_scalar(out=LBN[:, 1, :, :], in0=lbt[:], scalar1=0.25, scalar2=-0.25, op0=ALU.mult, op1=ALU.add)
        pg.op("dve", lb2, reads=[lbb], writes=[CONSTb])

        for l in range(4):
            for pc in range(4):
                i = (l * 4 + pc) % 2
                src = adaw_d[l].rearrange("(kc p) n -> p kc n", p=P)[:, :, pc * 768:(pc + 1) * 768]
                pg.op("sp", lambda e, i=i, src=src: e.dma_start(out=aw[i][:], in_=src), writes=[awb[i]], chan=awb[i])

                def mm(e, l=l, pc=pc, i=i):
                    ins = None
                    for oc in range(6):
                        o = pc * 6 + oc
                        for kc in range(KC):
                            ins = e.matmul(pada[l][:, o * 3:(o + 1) * 3], lhsT=aw[i][:, kc, oc * 128:(oc + 1) * 128],
                                           rhs=scT[:, kc, :], start=(kc == 0), stop=(kc == KC - 1))
                    return ins
                pg.op("pe", mm, reads=[awb[i], scb], writes=[padab[l]])

            def ev(e, l=l):
                return e.tensor_tensor(out=adaT[:, l, :, :], in0=pada[l][:, 0:72].rearrange("p (a b) -> p a b", b=3),
                                       in1=bc(adab[:, l, :].unsqueeze(2), [P, 24, 3]), op=ALU.add)
            pg.op("dve", ev, reads=[padab[l]] + all_ld, writes=[adaTb])

            def mod(e, l=l):
                e.tensor_copy(out=MOD[:, l, 0, :, :], in_=adaT[:, l, 0:8, :])
                e.tensor_copy(out=MOD[:, l, 2, :, :], in_=adaT[:, l, 16:24, :])
                return e.scalar_tensor_tensor(out=MOD[:, l, 1, :, :], in0=adaT[:, l, 8:16, :], scalar=1.0,
                                              in1=bc(normg[:, l, :].unsqueeze(2), [P, KC, 3]), op0=ALU.add, op1=ALU.mult)
            pg.op("dve", mod, reads=[adaTb] + all_ld, writes=[MODb])
            if l % 2 == 1:
                def bgf(e, l=l):
                    return e.tensor_tensor(out=BG[:, l, :, :], in0=adaT[:, l, 16:24, :],
                                           in1=bc(cbout[:, l // 2, :].unsqueeze(2), [P, KC, 3]), op=ALU.mult)
                pg.op("pool", bgf, reads=[adaTb, CONSTb] + all_ld, writes=[MODb])
        pg.flush(es)

    def tile_cols(t):
        return t * 512, 512

    def tile_j(t):
        return 2 if t == 8 else t // 4

    def xt_view(x_ap, t):
        c0, n = tile_cols(t)
        return x_ap.rearrange("(kc p) n -> p kc n", p=P)[:, :, c0:c0 + n]

    def phase_norm(l, src_i, tiles, final=False):
        with ExitStack() as es:
            xt = [sb(es, f"xt{i}", [P, KC, 512]) for i in range(2)]
            xtb = pg.bufs_n("nxt", 2)
            sq = sb(es, "sq", [P, KC, 512], BF16)
            sqb = pg.buf("sq")
            lnv = sb(es, "lnv", [P, 512])
            rstd = sb(es, "rstd", [P, 512])
            rsb = pg.buf("rstd")
            tmp = sb(es, "tmp", [P, KC, 512])
            tmpb = pg.buf("tmp")
            ot = [sb(es, f"ot{i}", [P, KC, 512]) for i in range(2)] if final else None
            otb = pg.bufs_n("ot", 2)
            pss = [ps(es, f"pss{i}", [P, 512]) for i in range(2)]
            pssb = pg.bufs_n("pss", 2)
            def loadn(n):
                t = tiles[n]
                i = n % 2
                pg.op("sp", lambda e, i=i, t=t: e.dma_start(out=xt[i][:], in_=xt_view(XT[src_i], t)),
                      reads=[XTb[src_i][t]], writes=[xtb[i]], chan=xtb[i])
            loadn(0)
            for n, t in enumerate(tiles):
                i = n % 2
                c0, ncol = tile_cols(t)
                j = tile_j(t)
                if n + 1 < len(tiles):
                    loadn(n + 1)
                pg.op("act", lambda e, i=i: e.activation(out=sq[:], in_=xt[i][:], func=AF.Square), reads=[xtb[i]], writes=[sqb])

                def mm(e, i=i):
                    ins = None
                    for kc in range(KC):
                        ins = e.matmul(pss[i][:], lhsT=onesb[:], rhs=sq[:, kc, :], start=(kc == 0), stop=(kc == KC - 1))
                    return ins
                pg.op("pe", mm, reads=[sqb, CONSTb], writes=[pssb[i]])
                pg.op("act", lambda e, i=i: e.activation(out=lnv[:], in_=pss[i][:], func=AF.Ln, scale=1.0 / D, bias=EPS),
                      reads=[pssb[i]], writes=[rsb])
                pg.op("act", lambda e: e.activation(out=rstd[:], in_=lnv[:], func=AF.Exp, scale=-0.5), reads=[rsb], writes=[rsb])
                pg.op("dve", lambda e, i=i: e.tensor_tensor(out=tmp[:], in0=xt[i][:], in1=bc(rstd[:].unsqueeze(1), [P, KC, 512]), op=ALU.mult),
                      reads=[xtb[i], rsb], writes=[tmpb])
                if not final:
                    def aff(e, c0=c0, j=j):
                        ins = None
                        for kc in range(KC):
                            ins = e.tensor_scalar(out=hT[:, kc, c0:c0 + 512], in0=tmp[:, kc, :], scalar1=MOD[:, l, 1, kc, j:j + 1],
                                                  scalar2=MOD[:, l, 0, kc, j:j + 1], op0=ALU.mult, op1=ALU.add)
                        return ins
                    pg.op("pool", aff, reads=[tmpb, MODb], writes=[hTb[t]])
                    if l % 2 == 1 or dbg:
                        pg.op("sp", lambda e, c0=c0: e.dma_start(out=HTd.rearrange("(kc p) n -> p kc n", p=P)[:, :, c0:c0 + 512],
                                                                   in_=hT[:, :, c0:c0 + 512]),
                              reads=[hTb[t]], writes=[HTdb[t]], chan=hTb[t])
                else:
                    def aff(e, i=i):
                        return e.tensor_tensor(out=ot[i][:], in0=tmp[:], in1=bc(fing[:].unsqueeze(2), [P, KC, 512]), op=ALU.mult)
                    pg.op("pool", aff, reads=[tmpb, CONSTb], writes=[otb[i]])
                    pg.op("sp", lambda e, i=i, c0=c0: e.dma_start(out=outT.rearrange("(kc p) n -> p kc n", p=P)[:, :, c0:c0 + 512], in_=ot[i][:]),
                          reads=[otb[i]], writes=[], chan=otb[i])
            pg.flush(es)

    def phase_out(l, w_d, src_b, tiles):
        with ExitStack() as es:
            w = sb(es, "wout", [P, EC, D], BF16)
            wb = pg.bufs_n("wout", 4)
            og = [sb(es, f"ogt{i}", [P, EC, 512], BF16) for i in range(2)]
            ogb = pg.bufs_n("ogt", 2)
            xt = [sb(es, f"oxt{i}", [P, KC, 512]) for i in range(2)]
            xtb = pg.bufs_n("oxt", 2)
            xn = xt
            xnb = xtb
            yt = sb(es, "yt", [P, 2, 512])
            ytb = pg.bufs_n("yt", 2)
            py = [ps(es, f"py{i}", [P, 512]) for i in range(4)]
            pyb = pg.bufs_n("py", 4)
            for q in range(4):
                src = w_d.rearrange("(ec p) n -> p ec n", p=P)[:, q * 4:(q + 1) * 4, :]
                pg.op("pool", lambda e, q=q, src=src: e.dma_start(out=w[:, q * 4:(q + 1) * 4, :], in_=src), writes=[wb[q]], chan=wb[q])
            cnt = 0

            def loads(n):
                t = tiles[n]
                i = n % 2
                c0, ncol = tile_cols(t)
                pg.op("sp", lambda e, i=i, c0=c0: e.dma_start(out=og[i][:], in_=OG.rearrange("(ec p) n -> p ec n", p=P)[:, :, c0:c0 + 512]),
                      reads=[bb[t] for bb in src_b], writes=[ogb[i]], chan=ogb[i])
                pg.op("sp", lambda e, i=i, t=t: e.dma_start(out=xt[i][:], in_=xt_view(XT[l], t)),
                      reads=[XTb[l][t]], writes=[xtb[i]], chan=xtb[i])
            loads(0)
            for n, t in enumerate(tiles):
                i = n % 2
                c0, ncol = tile_cols(t)
                j = tile_j(t)
                if n + 1 < len(tiles):
                    loads(n + 1)
                for dc in range(KC):
                    k = cnt % 4
                    k2 = cnt % 2
                    cnt += 1

                    def mm(e, i=i, dc=dc, k=k):
                        ins = None
                        for ec in range(EC):
                            ins = e.matmul(py[k][:], lhsT=w[:, ec, dc * 128:(dc + 1) * 128], rhs=og[i][:, ec, :],
                                           start=(ec == 0), stop=(ec == EC - 1))
                        return ins
                    pg.op("pe", mm, reads=[ogb[i]] + wb, writes=[pyb[k]])
                    pg.op("act", lambda e, k=k, k2=k2, dc=dc, j=j: e.activation(out=yt[:, k2, :], in_=py[k][:], func=AF.Identity,
                                                                           scale=MOD[:, l, 2, dc, j:j + 1], bias=BG[:, l, dc, j:j + 1]),
                          reads=[pyb[k], MODb], writes=[ytb[k2]])
                    pg.op("dve", lambda e, i=i, k2=k2, dc=dc: e.tensor_tensor(out=xn[i][:, dc, :], in0=yt[:, k2, :], in1=xt[i][:, dc, :], op=ALU.add),
                          reads=[ytb[k2], xtb[i]], writes=[xnb[i]])
                pg.op("sp", lambda e, i=i, t=t: e.dma_start(out=xt_view(XT[l + 1], t), in_=xn[i][:]),
                      reads=[xnb[i]], writes=[XTb[l + 1][t]], chan=xnb[i])
            pg.flush(es)

    TB = 64
    HB = 32
    NBK = 36
    U32 = mybir.dt.uint32

    def phase_hgrn(l):
        jl = l // 2
        with ExitStack() as es:
            Wset = [sb(es, f"W{i}", [P, 5, KC, P], BF16) for i in range(2)]
            Wbset = [pg.bufs_n(f"W{i}_", 5) for i in range(2)]
            QS = sb(es, "QS", [P, 2304])
            GT = sb(es, "GT", [P, 2304], BF16)
            OACC = sb(es, "OACC", [P, 2304])
            VT = sb(es, "VT", [TB, NBK, P], BF16)
            QT = [sb(es, f"QT{d}", [P, 2304], BF16) for d in range(2)]
            KT = [sb(es, f"KT{d}", [P, 2304], BF16) for d in range(2)]
            QH = [sb(es, f"QH{d}", [P, 2304], BF16) for d in range(2)]
            KH = [sb(es, f"KH{d}", [P, 2304], BF16) for d in range(2)]
            KCc = [sb(es, f"KC{d}", [P, NBK, HB], BF16) for d in range(2)]
            BB = [sb(es, f"BB{d}", [P, 512]) for d in range(2)]
            CQ = [sb(es, f"CQ{d}", [P, 2 * NBK]) for d in range(2)]
            CK = [sb(es, f"CK{d}", [P, 2 * NBK]) for d in range(2)]
            CC = [sb(es, f"CC{d}", [P, NBK]) for d in range(2)]
            DE = [sb(es, f"DE{d}", [P, NBK]) for d in range(2)]
            CT1 = [sb(es, f"CT1{d}", [P, 2 * NBK]) for d in range(2)]
            CT2 = [sb(es, f"CT2{d}", [P, NBK]) for d in range(2)]
            S32 = sb(es, "S32", [P, 2, P])
            SB = sb(es, "SB", [P, 2, 2, P], BF16)
            AT = sb(es, "AT", [TB, 3, 2, TB], BF16)
            KHT = sb(es, "KHT", [TB, 3, 2, P], BF16)
            TQ = sb(es, "TQ", [P, 512])
            TG = sb(es, "TG", [P, 512])
            TT = [sb(es, f"TTs{d}", [P, 512]) for d in range(2)]
            GG = [sb(es, "GGs", [P, 512])] * 2
            KK = [sb(es, "KKs", [P, 512])] * 2
            DDt = [sb(es, "DDs", [P, 512])] * 2
            EQ = [sb(es, "EQs", [P, 512])] * 2
            EK = DDt
            SQ = sb(es, "SQo", [P, 512], BF16)
            LNV = TQ
            RS = TG
            T1 = TT[0]
            OGt = [sb(es, "OGo", [P, 512], BF16)] * 2
            hng2 = sb(es, "hng2", [P, 1])
            pq = ps(es, "pq", [P, 512]); pgt = ps(es, "pgt", [P, 512]); pz = [ps(es, f"pz{d}", [P, 512]) for d in range(2)]
            pv = ps(es, "pv", [P, 512])
            pA = ps(es, "pA", [P, 3, 2, TB])
            pO = ps(es, "pO", [P, 2, 2, TB])
            pT = ps(es, "pT", [P, 3, 2, P], BF16)
            b_ = pg.buf
            QSb, GTb, VTb = b_("QS"), b_("GT"), b_("VT")
            OACCb = pg.bufs_n("OACC", NBK)
            QTb, KTb, QHb, KHb, KCb, BBb = (pg.bufs_n(n, 2) for n in ("QT", "KT", "QH", "KH", "KC", "BB"))
            CSTb = pg.bufs_n("CST", 2)
            S32b, SBb = (pg.bufs_n(n, 2) for n in ("S32", "SB"))
            ATb, KHTb = (pg.bufs_n(n, 3) for n in ("AT", "KHT"))
            TQb, TGb = b_("TQ"), b_("TG")
            GGb, KKb, DDb, EQb = ([pg.buf(n)] * 2 for n in ("GG", "KK", "DD", "EQ"))
            EKb = DDb
            TTb = pg.bufs_n("TT", 2)
            SQb = b_("SQo")
            OGtb = [pg.buf("OGo")] * 2
            pqb, pgtb, pvb = b_("pq"), b_("pgt"), b_("pv")
            pPb = [pg.bufs_n(f"pP{d}_", 4) for d in range(2)]
            pAb, pTb = (pg.bufs_n(n, 3) for n in ("pA", "pT"))
            pOb = pg.bufs_n("pO", 2)
            zb = b_("zero")

            def init(e):
                e.memset(AT[:].rearrange("p a b c -> p (a b c)"), 0.0)
                return e.tensor_scalar(out=hng2[:], in0=hng[:, jl:jl + 1], scalar1=0.5, scalar2=None, op0=ALU.mult)
            pg.op("dve", init, reads=[CONSTb], writes=ATb + [zb])
            ogcnt = 0
            def load_w(hh):
                for s in range(5):
                    col0 = s * E + hh * P
                    src = hwin_d[jl].rearrange("(kc p) n -> p kc n", p=P)[:, :, col0:col0 + P]
                    pg.op("pool", lambda e, s=s, src=src, hh=hh: e.dma_start(out=Wset[hh % 2][:, s, :, :], in_=src), writes=[Wbset[hh % 2][s]], chan=Wbset[hh % 2][s])
            load_w(0)
            for h in range(16):
                W = Wset[h % 2]
                Wb = Wbset[h % 2]
                Acol = [LBA[:, jl, d, h:h + 1] for d in range(2)]
                Bcol = [LBB[:, jl, d, h:h + 1] for d in range(2)]
                Ncol = [LBN[:, jl, d, h:h + 1] for d in range(2)]
                for b in range(2):
                    if b == 1 and h + 1 < 16:
                        load_w(h + 1)
                    ltiles = [(0, 256, NLAT + b * 256, 8)] + [(256 + i * 512, 512, b * 2048 + i * 512, b * 4 + i) for i in range(4)]
                    for (lc, n, gc, gt) in ltiles:
                        rd = [hTb[gt]]

                        def proj(e, s, dst, n=n, gc=gc, W=W):
                            ins = None
                            for kc in range(KC):
                                ins = e.matmul(dst[:, 0:n], lhsT=W[:, s, kc, :], rhs=hT[:, kc, gc:gc + n], start=(kc == 0), stop=(kc == KC - 1))
                            return ins
                        pg.op("pe", lambda e, f=proj: f(e, 0, pq), reads=rd + [Wb[0]], writes=[pqb])
                        pg.op("pe", lambda e, f=proj: f(e, 4, pgt), reads=rd + [Wb[4]], writes=[pgtb])
                        pg.op("pe", lambda e, f=proj: f(e, 2, pz[0]), reads=rd + [Wb[2]], writes=pPb[0])
                        pg.op("pe", lambda e, f=proj: f(e, 3, pz[1]), reads=rd + [Wb[3]], writes=pPb[1])
                        for v0 in range(0, n // TB, 4):
                            def projv(e, gc=gc, v0=v0, W=W):
                                ins = None
                                for bi in range(4):
                                    t0 = gc + (v0 + bi) * TB
                                    for kc in range(KC):
                                        ins = e.matmul(pv[0:TB, bi * P:(bi + 1) * P], lhsT=hT[:, kc, t0:t0 + TB], rhs=W[:, 1, kc, :],
                                                       start=(kc == 0), stop=(kc == KC - 1))
                                return ins
                            pg.op("pe", projv, reads=rd + [Wb[1]], writes=[pvb])
                            bl0 = lc // TB + v0
                            pg.op("dve", lambda e, bl0=bl0: e.tensor_copy(out=VT[:, bl0:bl0 + 4, :], in_=pv[0:TB, :].rearrange("p (a b) -> p a b", b=P)),
                                  reads=[pvb], writes=[VTb])
                        pg.op("act", lambda e, n=n: e.activation(out=TQ[:, 0:n], in_=pq[:, 0:n], func=AF.Tanh, scale=0.5), reads=[pqb], writes=[TQb])
                        pg.op("act", lambda e, n=n: e.activation(out=TG[:, 0:n], in_=pgt[:, 0:n], func=AF.Tanh, scale=0.5), reads=[pgtb], writes=[TGb])
                        for d in range(2):
                            pg.op("act", lambda e, n=n, d=d: e.activation(out=TT[d][:, 0:n], in_=pz[d][:, 0:n], func=AF.Tanh, scale=0.5),
                                  reads=pPb[d], writes=[TTb[d]])
                        pg.op("dve", lambda e, n=n, lc=lc: e.scalar_tensor_tensor(out=QS[:, lc:lc + n], in0=TQ[:, 0:n], scalar=1.0, in1=pq[:, 0:n], op0=ALU.add, op1=ALU.mult),
                              reads=[TQb, pqb], writes=[QSb])
                        pg.op("dve", lambda e, n=n, lc=lc: e.scalar_tensor_tensor(out=GT[:, lc:lc + n], in0=TG[:, 0:n], scalar=1.0, in1=pgt[:, 0:n], op0=ALU.add, op1=ALU.mult),
                              reads=[TGb, pgtb], writes=[GTb])
                        for d in range(2):
                            nh = n // HB
                            nb = n // TB
                            h0 = lc // HB
                            b0 = lc // TB
                            rpos = 15 if d == 0 else 16
                            pg.op("act", lambda e, n=n, d=d: e.activation(out=GG[d][:, 0:n], in_=TT[d][:, 0:n], func=AF.Ln, scale=Bcol[d], bias=Acol[d]),
                                  reads=[TTb[d], CONSTb], writes=[GGb[d]])
                            pg.op("pool", lambda e, n=n, d=d: e.tensor_scalar(out=KK[d][:, 0:n], in0=TT[d][:, 0:n], scalar1=Ncol[d], scalar2=Bcol[d], op0=ALU.mult, op1=ALU.add),
                                  reads=[TTb[d], CONSTb], writes=[KKb[d]])

                            def scan(e, n=n, d=d):
                                if d == 0:
                                    return e.tensor_tensor_scan(out=BB[d][:, 0:n], data0=resets[:, 0, 0:n], data1=GG[d][:, 0:n], initial=0.0, op0=ALU.mult, op1=ALU.add)
                                return e.tensor_tensor_scan(out=BB[d][:, 0:n][:, ::-1], data0=resets[:, 1, 0:n][:, ::-1], data1=GG[d][:, 0:n][:, ::-1],
                                                            initial=0.0, op0=ALU.mult, op1=ALU.add)
                            pg.op("dve", scan, reads=[GGb[d], CONSTb], writes=[BBb[d]])
                            BBv = BB[d][:, 0:n].rearrange("p (a b) -> p a b", b=HB)

                            def dd(e, n=n, d=d, BBv=BBv, nh=nh, rpos=rpos):
                                return e.tensor_tensor(out=DDt[d][:, 0:n].rearrange("p (a b) -> p a b", b=HB), in0=BBv,
                                                       in1=bc(BBv[:, :, rpos:rpos + 1], [P, nh, HB]), op=ALU.subtract)
                            pg.op("dve", dd, reads=[BBb[d]], writes=[DDb[d]])
                            pg.op("act", lambda e, n=n, d=d: e.activation(out=EQ[d][:, 0:n], in_=DDt[d][:, 0:n], func=AF.Exp), reads=[DDb[d]], writes=[EQb[d]])
                            pg.op("act", lambda e, n=n, d=d: e.activation(out=EK[d][:, 0:n], in_=DDt[d][:, 0:n], func=AF.Exp, scale=-1.0), reads=[DDb[d]], writes=[EKb[d]])
                            pg.op("dve", lambda e, n=n, d=d, lc=lc: e.scalar_tensor_tensor(out=QT[d][:, lc:lc + n], in0=QS[:, lc:lc + n], scalar=0.5, in1=EQ[d][:, 0:n], op0=ALU.mult, op1=ALU.mult),
                                  reads=[QSb, EQb[d]], writes=[QTb[d]])
                            pg.op("pool", lambda e, n=n, d=d, lc=lc: e.tensor_tensor(out=KT[d][:, lc:lc + n], in0=KK[d][:, 0:n], in1=EK[d][:, 0:n], op=ALU.mult),
                                  reads=[KKb[d], EKb[d]], writes=[KTb[d]])
                            BB4 = BB[d][:, 0:n].rearrange("p (a h b) -> p a h b", h=2, b=HB)
                            fh, sh = (0, 1) if d == 0 else (1, 0)
                            eh, ep = (1, HB - 1) if d == 0 else (0, 0)

                            def cst(e, d=d, BB4=BB4, nb=nb, nh=nh, h0=h0, b0=b0, rpos=rpos, fh=fh, sh=sh, eh=eh, ep=ep):
                                e.tensor_tensor(out=CT1[d][:, h0:h0 + nh].rearrange("p (a h) -> p a h", h=2),
                                                in0=bc(BB4[:, :, eh, ep:ep + 1], [P, nb, 2]), in1=BB4[:, :, :, rpos], op=ALU.subtract)
                                return e.tensor_tensor(out=CT2[d][:, b0:b0 + nb], in0=BB4[:, :, sh, rpos], in1=BB4[:, :, fh, rpos], op=ALU.subtract)
                            pg.op("dve", cst, reads=[BBb[d]], writes=[CSTb[d]])

                            def cst2(e, d=d, BB4=BB4, nb=nb, nh=nh, h0=h0, b0=b0, rpos=rpos, eh=eh, ep=ep):
                                e.activation(out=CQ[d][:, h0:h0 + nh].rearrange("p (a h) -> p a h", h=2), in_=BB4[:, :, :, rpos], func=AF.Exp)
                                e.activation(out=CK[d][:, h0:h0 + nh], in_=CT1[d][:, h0:h0 + nh], func=AF.Exp)
                                e.activation(out=CC[d][:, b0:b0 + nb], in_=CT2[d][:, b0:b0 + nb], func=AF.Exp)
                                return e.activation(out=DE[d][:, b0:b0 + nb], in_=BB4[:, :, eh, ep], func=AF.Exp)
                            pg.op("act", cst2, reads=[BBb[d], CSTb[d]], writes=[CSTb[d]])
                            pg.op("pool", lambda e, n=n, d=d, lc=lc, nh=nh, h0=h0: e.tensor_tensor(
                                out=QH[d][:, lc:lc + n].rearrange("p (a b) -> p a b", b=HB), in0=QT[d][:, lc:lc + n].rearrange("p (a b) -> p a b", b=HB),
                                in1=bc(CQ[d][:, h0:h0 + nh].unsqueeze(2), [P, nh, HB]), op=ALU.mult), reads=[QTb[d], CSTb[d]], writes=[QHb[d]])
                            pg.op("pool", lambda e, n=n, d=d, lc=lc, nh=nh, h0=h0: e.tensor_tensor(
                                out=KH[d][:, lc:lc + n].rearrange("p (a b) -> p a b", b=HB), in0=KT[d][:, lc:lc + n].rearrange("p (a b) -> p a b", b=HB),
                                in1=bc(CK[d][:, h0:h0 + nh].unsqueeze(2), [P, nh, HB]), op=ALU.mult), reads=[KTb[d], CSTb[d]], writes=[KHb[d]])
                            pg.op("pool", lambda e, n=n, d=d, lc=lc, nb=nb, b0=b0, fh=fh: e.tensor_tensor(
                                out=KCc[d][:, b0:b0 + nb, :], in0=KT[d][:, lc:lc + n].rearrange("p (a h b) -> p a h b", h=2, b=HB)[:, :, fh, :],
                                in1=bc(CC[d][:, b0:b0 + nb].unsqueeze(2), [P, nb, HB]), op=ALU.mult), reads=[KTb[d], CSTb[d]], writes=[KCb[d]])
                    pg.op("dve", lambda e: e.memset(S32[:].rearrange("p a b -> p (a b)"), 0.0), writes=S32b)
                    pg.op("pool", lambda e: e.memset(SB[:, 0, :, :], 0.0), writes=[SBb[0]])
                    order = [list(range(NBK)), [3, 2, 1, 0] + list(range(NBK - 1, 3, -1))]
                    seen = set()
                    F0s = (0, HB)
                    S0s = (HB, 0)

                    def need_o(step):
                        return True

                    def stage_a1(step):
                        par = step % 3
                        blks = [order[d][step] for d in range(2)]
                        if need_o(step):
                            def sc(e, par=par, blks=blks):
                                ins = None
                                for d in range(2):
                                    c0 = blks[d] * TB
                                    F0, S0 = F0s[d], S0s[d]
                                    e.matmul(pA[F0:F0 + HB, par, d, F0:F0 + HB], lhsT=KT[d][:, c0 + F0:c0 + F0 + HB], rhs=QT[d][:, c0 + F0:c0 + F0 + HB], start=True, stop=True)
                                    e.matmul(pA[S0:S0 + HB, par, d, S0:S0 + HB], lhsT=KT[d][:, c0 + S0:c0 + S0 + HB], rhs=QT[d][:, c0 + S0:c0 + S0 + HB], start=True, stop=True)
                                    ins = e.matmul(pA[F0:F0 + HB, par, d, S0:S0 + HB], lhsT=KCc[d][:, blks[d], :], rhs=QT[d][:, c0 + S0:c0 + S0 + HB], start=True, stop=True)
                                return ins
                            pg.op("pe", sc, reads=KTb + QTb + KCb + [zb], writes=[pAb[par]])
                            pg.op("dve", lambda e, par=par: e.copy_predicated(out=AT[:, par, :, :], mask=masks[0:TB, :, 0:TB].bitcast(U32), data=pA[0:TB, par, :, :]),
                                  reads=[pAb[par], CONSTb, zb], writes=[ATb[par]])
                        if step < NBK - 1:
                            def tr(e, par=par, blks=blks):
                                ins = None
                                for d in range(2):
                                    c0 = blks[d] * TB
                                    ins = e.transpose(out=pT[0:TB, par, d, :], in_=KH[d][:, c0:c0 + TB], identity=identb[:])
                                return ins
                            pg.op("pe", tr, reads=KHb + [CONSTb], writes=[pTb[par]])
                            pg.op("act", lambda e, par=par: e.activation(out=KHT[:, par, :, :], in_=pT[0:TB, par, :, :], func=AF.Copy), reads=[pTb[par]], writes=[KHTb[par]])

                    def stage_a2(step):
                        if step >= NBK - 1:
                            return
                        par = step % 3
                        k = step % 4
                        blks = [order[d][step] for d in range(2)]

                        def pm(e, par=par, k=k, blks=blks):
                            ins = None
                            for d in range(2):
                                ins = e.matmul(pz[d][:, k * P:(k + 1) * P], lhsT=KHT[:, par, d, :], rhs=VT[:, blks[d], :], start=True, stop=True)
                            return ins
                        pg.op("pe", pm, reads=[KHTb[par], VTb], writes=[pPb[0][k], pPb[1][k]])

                    def stage_om(step):
                        if not need_o(step):
                            return
                        par = step % 2
                        blks = [order[d][step] for d in range(2)]

                        def om(e, par=par, blks=blks, step=step):
                            ins = None
                            for d in range(2):
                                c0 = blks[d] * TB
                                e.matmul(pO[:, par, d, :], lhsT=SB[:, par, d, :], rhs=QH[d][:, c0:c0 + TB], start=True, stop=False)
                                ins = e.matmul(pO[:, par, d, :], lhsT=VT[:, blks[d], :], rhs=AT[:, step % 3, d, :], start=False, stop=True)
                            return ins
                        pg.op("pe", om, reads=[SBb[par], ATb[step % 3], VTb] + QHb, writes=[pOb[par]])

                    def stage_upd(step):
                        if step >= NBK - 1:
                            return
                        k = step % 4
                        blks = [order[d][step] for d in range(2)]
                        for d in range(2):
                            pg.op("dve", lambda e, d=d, k=k, blk=blks[d]: e.scalar_tensor_tensor(out=S32[:, d, :], in0=S32[:, d, :], scalar=DE[d][:, blk:blk + 1], in1=pz[d][:, k * P:(k + 1) * P], op0=ALU.mult, op1=ALU.add),
                                  reads=[S32b[d], CSTb[d], pPb[d][k]], writes=[S32b[d]])
                        npar = (step + 1) % 2
                        pg.op("pool", lambda e, npar=npar: e.tensor_copy(out=SB[:, npar, :, :], in_=S32[:]), reads=S32b, writes=[SBb[npar]])

                    def stage_evac(step):
                        if not need_o(step):
                            return
                        par = step % 2
                        for d in range(2):
                            blk = order[d][step]
                            c0 = blk * TB
                            if blk not in seen:
                                seen.add(blk)
                                pg.op("act", lambda e, c0=c0, d=d, par=par: e.activation(out=OACC[:, c0:c0 + TB], in_=pO[:, par, d, :], func=AF.Copy),
                                      reads=[pOb[par]], writes=[OACCb[blk]])
                            else:
                                pg.op("dve", lambda e, c0=c0, d=d, par=par: e.tensor_tensor(out=OACC[:, c0:c0 + TB], in0=pO[:, par, d, :], in1=OACC[:, c0:c0 + TB], op=ALU.add),
                                      reads=[pOb[par], OACCb[blk]], writes=[OACCb[blk]])

                    stage_a1(0)
                    stage_a2(0)
                    stage_a1(1)
                    stage_a2(1)
                    for step in range(NBK):
                        if step + 2 < NBK:
                            stage_a1(step + 2)
                        stage_om(step)
                        if step + 2 < NBK:
                            stage_a2(step + 2)
                        stage_upd(step)
                        stage_evac(step)
                    for (lc, n, gc, gt) in ltiles:
                        i = ogcnt % 2
                        ogcnt += 1
                        pg.op("act", lambda e, n=n, lc=lc: e.activation(out=SQ[:, 0:n], in_=OACC[:, lc:lc + n], func=AF.Square), reads=OACCb[lc // TB:(lc + n) // TB], writes=[SQb])
                        pg.op("pe", lambda e, n=n: e.matmul(pq[:, 0:n], lhsT=onesb[:], rhs=SQ[:, 0:n], start=True, stop=True),
                              reads=[SQb, CONSTb], writes=[pqb])
                        pg.op("act", lambda e, n=n: e.activation(out=LNV[:, 0:n], in_=pq[:, 0:n], func=AF.Ln, scale=1.0 / P, bias=EPS), reads=[pqb], writes=[TQb])
                        pg.op("act", lambda e, n=n: e.activation(out=RS[:, 0:n], in_=LNV[:, 0:n], func=AF.Exp, scale=-0.5), reads=[TQb], writes=[TGb])
                        pg.op("dve", lambda e, n=n, lc=lc: e.scalar_tensor_tensor(out=T1[:, 0:n], in0=OACC[:, lc:lc + n], scalar=hng2[:, 0:1], in1=RS[:, 0:n], op0=ALU.mult, op1=ALU.mult),
                              reads=OACCb[lc // TB:(lc + n) // TB] + [TGb, zb], writes=[TTb[0]])
                        pg.op("pool", lambda e, n=n, lc=lc, i=i: e.tensor_tensor(out=OGt[i][:, 0:n], in0=T1[:, 0:n], in1=GT[:, lc:lc + n], op=ALU.mult),
                              reads=[TTb[0], GTb], writes=[OGtb[i]])
                        pg.op("sp", lambda e, n=n, gc=gc, i=i, h=h: e.dma_start(out=OG[h * P:(h + 1) * P, gc:gc + n], in_=OGt[i][:, 0:n]),
                              reads=[OGtb[i]], writes=[OGb[h][gt]], chan=OGtb[i])
            pg.flush(es)

    def phase_conv1(l):
        jl = l // 2
        vertical = (jl % 2 == 1)
        with_ctx = (l == 1)
        with ExitStack() as es:
            W = [sb(es, f"cW{i}", [P, 2, KC, P], BF16) for i in range(2)]
            Wb = [pg.bufs_n(f"cW{i}_", 2) for i in range(2)]
            dg = [sb(es, f"dg{i}", [P, 31, P], BF16) for i in range(2)]
            dgb = pg.bufs_n("dg", 2)
            if vertical:
                up = [sb(es, f"up{b}", [P, 62, 64], BF16) for b in range(2)]
            else:
                up = [sb(es, f"up{b}", [P, 32, 94], BF16) for b in range(2)]
            upb = pg.bufs_n("up", 2)
            upc = sb(es, "upc", [P, 2, 286], BF16)
            upcb = pg.buf("upc")
            sg = [sb(es, f"sg{i}", [P, 512]) for i in range(2)]
            sgb = pg.bufs_n("sg", 2)
            uo = [sb(es, f"uo{i}", [P, 512]) for i in range(2)]
            uob = pg.bufs_n("uo", 2)
            pa = [ps(es, f"pa{i}", [P, 512]) for i in range(2)]
            pgl = [ps(es, f"pgl{i}", [P, 512]) for i in range(2)]
            pc = [ps(es, f"pc{i}", [P, 512]) for i in range(2)]
            pab, pglb, pcb = pg.bufs_n("pa", 2), pg.bufs_n("pgl", 2), pg.bufs_n("pc", 2)

            def init(e):
                for b in range(2):
                    e.memset(up[b][:].rearrange("p a b -> p (a b)"), 0.0)
                return e.memset(upc[:].rearrange("p a b -> p (a b)"), 0.0)
            pg.op("pool", init, writes=upb + [upcb])
            tcnt = 0
            ccnt = 0
            def load_chunk(c):
                wi = c % 2
                for s in range(2):
                    col0 = s * E + c * P
                    src = cwin_d[jl].rearrange("(kc p) n -> p kc n", p=P)[:, :, col0:col0 + P]
                    pg.op("pool", lambda e, s=s, wi=wi, src=src: e.dma_start(out=W[wi][:, s, :, :], in_=src), writes=[Wb[wi][s]], chan=Wb[wi][s])
                pg.op("pool", lambda e, c=c, wi=wi: e.tensor_tensor(out=dg[wi][:], in0=bc(identb[:].unsqueeze(1), [P, 31, P]),
                                                                    in1=bc(cdw[:, jl, c, :].unsqueeze(2), [P, 31, P]), op=ALU.mult),
                      reads=[CONSTb], writes=[dgb[wi]])
            load_chunk(0)
            load_chunk(1)
            for c in range(16):
                wi = c % 2
                if c >= 1 and c + 1 < 16:
                    load_chunk(c + 1)
                ba = cbin[:, jl, c:c + 1]
                bgl = cbin[:, jl, 16 + c:17 + c]
                segs = [(b, i) for b in range(2) for i in range(4)] + ([(2, 0)] if with_ctx else [])
                for (b, i) in segs:
                    k = tcnt % 2
                    tcnt += 1
                    if b < 2:
                        gc, gt = b * 2048 + i * 512, b * 4 + i
                    else:
                        gc, gt = NLAT, 8

                    def proj(e, s, dst, gc=gc, wi=wi):
                        ins = None
                        for kc in range(KC):
                            ins = e.matmul(dst[:], lhsT=W[wi][:, s, kc, :], rhs=hT[:, kc, gc:gc + 512], start=(kc == 0), stop=(kc == KC - 1))
                        return ins
                    pg.op("pe", lambda e, f=proj, k=k: f(e, 0, pa[k]), reads=[hTb[gt], Wb[wi][0]], writes=[pab[k]])
                    pg.op("pe", lambda e, f=proj, k=k: f(e, 1, pgl[k]), reads=[hTb[gt], Wb[wi][1]], writes=[pglb[k]])
                    pg.op("act", lambda e, k=k, bgl=bgl: e.activation(out=sg[k][:], in_=pgl[k][:], func=AF.Sigmoid, bias=bgl, scale=1.0),
                          reads=[pglb[k], CONSTb], writes=[sgb[k]])
                    if b < 2:
                        if vertical:
                            dst = up[b][:, 15 + i * 8:15 + i * 8 + 8, :]
                        else:
                            dst = up[b][:, i * 8:i * 8 + 8, 15:79]
                        pg.op("dve", lambda e, k=k, dst=dst, ba=ba: e.scalar_tensor_tensor(out=dst, in0=pa[k][:].rearrange("p (a b) -> p a b", b=64), scalar=ba,
                                                                                             in1=sg[k][:].rearrange("p (a b) -> p a b", b=64), op0=ALU.add, op1=ALU.mult),
                              reads=[pab[k], sgb[k], CONSTb], writes=[upb[b]])
                    else:
                        dst = upc[:, :, 15:271]
                        pg.op("dve", lambda e, k=k, dst=dst, ba=ba: e.scalar_tensor_tensor(out=dst, in0=pa[k][:].rearrange("p (a b) -> p a b", b=256), scalar=ba,
                                                                                             in1=sg[k][:].rearrange("p (a b) -> p a b", b=256), op0=ALU.add, op1=ALU.mult),
                              reads=[pab[k], sgb[k], CONSTb], writes=[upcb])
                for (b, i) in segs:
                    k = ccnt % 2
                    ccnt += 1
                    if b < 2:
                        gc, gt = b * 2048 + i * 512, b * 4 + i
                    else:
                        gc, gt = NLAT, 8

                    def conv(e, b=b, i=i, k=k, wi=wi):
                        taps = []
                        for j in range(31):
                            if b < 2 and vertical:
                                lo, hi = i * 8 + j - 15, i * 8 + 7 + j - 15
                                if hi < 0 or lo > 31:
                                    continue
                            taps.append(j)
                        ins = None
                        for n_, j in enumerate(taps):
                            if b == 2:
                                rhs = upc[:, :, j:j + 256]
                                out = pc[k][:].rearrange("p (a b) -> p a b", b=256)
                            elif vertical:
                                rhs = up[b][:, i * 8 + j:i * 8 + j + 8, :]
                                out = pc[k][:].rearrange("p (a b) -> p a b", b=64)
                            else:
                                rhs = up[b][:, i * 8:i * 8 + 8, j:j + 64]
                                out = pc[k][:].rearrange("p (a b) -> p a b", b=64)
                            ins = e.matmul(out, lhsT=dg[wi][:, j, :], rhs=rhs, start=(n_ == 0), stop=(n_ == len(taps) - 1))
                        return ins
                    pg.op("pe", conv, reads=[upb[b] if b < 2 else upcb, dgb[wi]], writes=[pcb[k]])
                    pg.op("act", lambda e, k=k, c=c: e.activation(out=uo[k][:], in_=pc[k][:], func=AF.Identity, bias=cdwb[:, jl, c:c + 1], scale=1.0),
                          reads=[pcb[k], CONSTb], writes=[uob[k]])
                    pg.op("sp", lambda e, k=k, c=c, gc=gc: e.dma_start(out=UC[c * P:(c + 1) * P, gc:gc + 512], in_=uo[k][:]),
                          reads=[uob[k]], writes=[UCb[c][gt]], chan=uob[k])
            pg.flush(es)

    def phase_conv2(l):
        jl = l // 2
        tiles = list(range(9)) if l == 1 else list(range(8))
        with ExitStack() as es:
            Wg = sb(es, "Wg", [P, KC, E], BF16)
            Wgb = pg.bufs_n("Wg", 4)
            uc = [sb(es, f"uct{i}", [P, EC, 256]) for i in range(2)]
            ucb = pg.bufs_n("uct", 2)
            ht = [sb(es, f"htt{i}", [P, KC, 256], BF16) for i in range(2)]
            htb = pg.bufs_n("htt", 2)
            sq = sb(es, "csq", [P, EC, 256])
            sqb = pg.buf("csq")
            mean = sb(es, "mean", [P, 256]); msq = sb(es, "msq", [P, 256]); var = sb(es, "var", [P, 256])
            lnv = sb(es, "clnv", [P, 256]); rstd = sb(es, "crstd", [P, 256]); mr = sb(es, "mr", [P, 256])
            stb = pg.buf("stats")
            xna = sb(es, "cxna", [P, EC, 256])
            s1a = xna
            s2a = sq
            xnab = pg.buf("cxna")
            s2ab = sqb
            s1ab = xnab
            ogt = [sb(es, "cog", [P, EC, 256], BF16)] * 2
            ogtb = [pg.buf("cog")] * 2
            p1 = ps(es, "p1", [P, 512]); p2 = ps(es, "p2", [P, 512])
            p1b, p2b = pg.buf("p1"), pg.buf("p2")
            pgp = [ps(es, f"pgp{i}", [P, 512]) for i in range(4)]
            pgpb = pg.bufs_n("pgp", 4)
            for q in range(4):
                src = cwin_d[jl].rearrange("(kc p) n -> p kc n", p=P)[:, q * 2:(q + 1) * 2, 2 * E:3 * E]
                pg.op("pool", lambda e, q=q, src=src: e.dma_start(out=Wg[:, q * 2:(q + 1) * 2, :], in_=src), writes=[Wgb[q]], chan=Wgb[q])
            n = 0
            gcnt = 0
            halves = [(t, hf) for t in tiles for hf in range(2)]

            def loads2(m):
                t, hf = halves[m]
                i = m % 2
                c0 = t * 512 + hf * 256
                pg.op("sp", lambda e, i=i, c0=c0: e.dma_start(out=uc[i][:], in_=UC.rearrange("(ec p) n -> p ec n", p=P)[:, :, c0:c0 + 256]),
                      reads=[UCb[c][t] for c in range(16)], writes=[ucb[i]], chan=ucb[i])
                pg.op("sp", lambda e, i=i, c0=c0: e.dma_start(out=ht[i][:], in_=HTd.rearrange("(kc p) n -> p kc n", p=P)[:, :, c0:c0 + 256]),
                      reads=[HTdb[t]], writes=[htb[i]], chan=htb[i])
            loads2(0)
            for t in tiles:
                for hf in range(2):
                    i = n % 2
                    n += 1
                    c0 = t * 512 + hf * 256
                    if n < len(halves):
                        loads2(n)
                    pg.op("act", lambda e, i=i: e.activation(out=sq[:], in_=uc[i][:], func=AF.Square), reads=[ucb[i]], writes=[sqb])

                    def st1(e, i=i):
                        ins = None
                        for ec in range(EC):
                            ins = e.matmul(p1[:, 0:256], lhsT=onesf[:], rhs=uc[i][:, ec, :], start=(ec == 0), stop=(ec == EC - 1))
                        return ins

                    def st2(e):
                        ins = None
                        for ec in range(EC):
                            ins = e.matmul(p2[:, 0:256], lhsT=onesf[:], rhs=sq[:, ec, :], start=(ec == 0), stop=(ec == EC - 1))
                        return ins
                    pg.op("pe", st1, reads=[ucb[i], CONSTb], writes=[p1b])
                    pg.op("pe", st2, reads=[sqb, CONSTb], writes=[p2b])

                    pg.op("dve", lambda e: e.tensor_scalar(out=mean[:], in0=p1[:, 0:256], scalar1=1.0 / E, scalar2=None, op0=ALU.mult), reads=[p1b], writes=[stb])
                    pg.op("dve", lambda e: e.tensor_tensor(out=msq[:], in0=mean[:], in1=mean[:], op=ALU.mult), reads=[stb], writes=[stb])
                    pg.op("dve", lambda e: e.scalar_tensor_tensor(out=var[:], in0=p2[:, 0:256], scalar=1.0 / E, in1=msq[:], op0=ALU.mult, op1=ALU.subtract),
                          reads=[stb, p2b], writes=[stb])
                    pg.op("act", lambda e: e.activation(out=lnv[:], in_=var[:], func=AF.Ln, bias=EPS, scale=1.0), reads=[stb], writes=[stb])
                    pg.op("act", lambda e: e.activation(out=rstd[:], in_=lnv[:], func=AF.Exp, scale=-0.5), reads=[stb], writes=[stb])
                    pg.op("dve", lambda e: e.tensor_tensor(out=mr[:], in0=mean[:], in1=rstd[:], op=ALU.mult), reads=[stb], writes=[stb])
                    pg.op("dve", lambda e, i=i: e.tensor_tensor(out=xna[:], in0=uc[i][:], in1=bc(rstd[:].unsqueeze(1), [P, EC, 256]), op=ALU.mult),
                          reads=[ucb[i], stb], writes=[xnab])
                    pg.op("pool", lambda e: e.tensor_tensor(out=xna[:], in0=xna[:], in1=bc(mr[:].unsqueeze(1), [P, EC, 256]), op=ALU.subtract),
                          reads=[xnab, stb], writes=[xnab])
                    for ec in range(EC):
                        k = gcnt % 4
                        gcnt += 1

                        def gp(e, i=i, ec=ec, k=k):
                            ins = None
                            for kc in range(KC):
                                ins = e.matmul(pgp[k][:, 0:256], lhsT=Wg[:, kc, ec * P:(ec + 1) * P], rhs=ht[i][:, kc, :], start=(kc == 0), stop=(kc == KC - 1))
                            return ins
                        pg.op("pe", gp, reads=[htb[i]] + Wgb, writes=[pgpb[k]])
                        pg.op("act", lambda e, k=k, ec=ec: e.activation(out=s2a[:, ec, :], in_=pgp[k][:, 0:256], func=AF.Silu, bias=cbin[:, jl, 32 + ec:33 + ec], scale=1.0),
                              reads=[pgpb[k], CONSTb], writes=[s2ab])
                    for ec in range(EC):
                        pg.op("act", lambda e, ec=ec: e.activation(out=s1a[:, ec, :], in_=xna[:, ec, :], func=AF.Silu, scale=clng[:, jl, ec:ec + 1], bias=clnb[:, jl, ec:ec + 1]),
                              reads=[xnab, CONSTb], writes=[s1ab])
                    pg.op("dve", lambda e, i=i: e.tensor_tensor(out=ogt[i][:], in0=s1a[:], in1=s2a[:], op=ALU.mult),
                          reads=[s1ab, s2ab], writes=[ogtb[i]])
                    pg.op("sp", lambda e, i=i, c0=c0: e.dma_start(out=OG.rearrange("(ec p) n -> p ec n", p=P)[:, :, c0:c0 + 256], in_=ogt[i][:]),
                          reads=[ogtb[i]], writes=[OGb[hf][t]], chan=ogtb[i])
            pg.flush(es)

    for l in range(n_layers):
        tiles_in = list(range(9)) if l < 3 else list(range(8))
        tiles_out = list(range(9)) if l < 2 else list(range(8))
        phase_norm(l, l, tiles_in)
        if l % 2 == 0:
            if "hgrn" not in SKIP:
                phase_hgrn(l)
                phase_out(l, hwout_d[l // 2], OGb, tiles_out)
        else:
            if "conv1" not in SKIP:
                phase_conv1(l)
            if "conv2" not in SKIP:
                phase_conv2(l)
            if "cout" not in SKIP:
                phase_out(l, cwout_d[l // 2], OGb[0:2], tiles_out)
    if dbg:
        pass
    phase_norm(0, n_layers, list(range(8)), final=True)
    top.close()
    return nc


_CACHE = {}


def _prep_inputs(inp):
    f = lambda a: np.ascontiguousarray(np.asarray(a, dtype=np.float32))
    x = f(inp["x"]); c = f(inp["c"]); ctx = f(inp["ctx"]); c_ctx = f(inp["c_ctx"])

    def colT(v, nch):
        return np.ascontiguousarray(v.reshape(nch, P).T)
    shared = {
        "normgT": np.ascontiguousarray(np.stack([colT(f(inp["norm_g"])[l], KC) for l in range(4)], axis=1).reshape(P, 4 * KC)),
        "fingT": colT(f(inp["final_norm_g"]), KC),
        "ada_w": f(inp["ada_w"]),
        "adabT": np.ascontiguousarray(np.stack([colT(f(inp["ada_b"])[l], 24) for l in range(4)], axis=1).reshape(P, 96)),
        "hgrn_w_in": f(inp["hgrn_w_in"]),
        "hgrn_w_out": f(inp["hgrn_w_out"]),
        "lbT": np.ascontiguousarray(f(inp["hgrn_lb_logits"]).reshape(2, 2, 16, P).transpose(3, 0, 1, 2).reshape(P, 64)),
        "hngT": np.ascontiguousarray(f(inp["hgrn_norm_g"]).T),
        "conv_w_in": f(inp["conv_w_in"]),
        "conv_w_out": f(inp["conv_w_out"]),
        "cbinT": np.ascontiguousarray(np.stack([colT(f(inp["conv_b_in"])[j], 48) for j in range(2)], axis=1).reshape(P, 96)),
        "cdwT": np.ascontiguousarray(f(inp["conv_dw"]).reshape(2, 31, 16, P).transpose(3, 0, 2, 1).reshape(P, 2 * 16 * 31)),
        "cdwbT": np.ascontiguousarray(np.stack([colT(f(inp["conv_dw_b"])[j], 16) for j in range(2)], axis=1).reshape(P, 32)),
        "clngT": np.ascontiguousarray(np.stack([colT(f(inp["conv_ln_g"])[j], 16) for j in range(2)], axis=1).reshape(P, 32)),
        "clnbT": np.ascontiguousarray(np.stack([colT(f(inp["conv_ln_b"])[j], 16) for j in range(2)], axis=1).reshape(P, 32)),
        "cboutT": np.ascontiguousarray(np.stack([colT(f(inp["conv_b_out"])[j], 8) for j in range(2)], axis=1).reshape(P, 16)),
        "ident": np.eye(P, dtype=np.float32),
    }
    s = np.arange(P)[:, None]
    t = np.arange(P)[None, :]
    shared["masks"] = np.ascontiguousarray(np.concatenate([(s <= t), (s >= t)], axis=1).astype(np.float32))
    r = np.ones((P, 2, 512), np.float32)
    r[:, 0, 0::64] = 0.0
    r[:, 1, 63::64] = 0.0
    shared["resets"] = np.ascontiguousarray(r.reshape(P, 1024))
    maps = []
    for k in range(8):
        b0, b1 = 2 * k, 2 * k + 1
        xT = np.empty((D, NT), np.float32)
        xT[:, 0:2048] = x[b0].T
        xT[:, 2048:4096] = x[b1].T
        xT[:, 4096:4352] = ctx[b0].T
        xT[:, 4352:4608] = ctx[b1].T
        cv = np.stack([c[b0], c[b1], c_ctx], axis=1)
        cT = np.ascontiguousarray(cv.reshape(KC, P, 3).transpose(1, 0, 2).reshape(P, KC * 3))
        m = dict(shared)
        m["xT0"] = xT
        m["cT"] = cT
        maps.append(m)
    return maps


def kernel(**inputs):
    if "nc" not in _CACHE:
        _CACHE["nc"] = build()
    nc = _CACHE["nc"]
    maps = _prep_inputs(inputs)
    res = run_bass_kernel_spmd(nc, maps, core_ids=list(range(8)))
    out = np.empty((16, 2048, D), np.float32)
    for k in range(8):
        oT = np.asarray(res.results[k]["outT"])
        out[2 * k] = oT[:, 0:2048].T
        out[2 * k + 1] = oT[:, 2048:4096].T
    return out
```

```python
from contextlib import ExitStack
import numpy as np
import concourse.bass as bass
import concourse.mybir as mybir
from concourse.bass_utils import run_bass_kernel_spmd

F32 = mybir.dt.float32
BF16 = mybir.dt.bfloat16
AF = mybir.ActivationFunctionType
ALU = mybir.AluOpType

P = 128
D = 1024
KC = 8
E = 2048
EC = 16
NT = 4608
NLAT = 4096
EPS = 1e-6
SKIP = set()
ENGS = ("pe", "act", "dve", "pool", "sp")


class Buf:
    __slots__ = ("name", "w", "r", "sem", "val")

    def __init__(self, name):
        self.name = name
        self.w = None
        self.r = []
        self.sem = None
        self.val = 0


class Op:
    __slots__ = ("eng", "fn", "deps", "sig", "seq", "dma", "chan", "val")


class Prog:
    def __init__(self, nc):
        self.nc = nc
        self.ops = {e: [] for e in ENGS}
        self.bufs = []
        self.all = []

    def buf(self, name):
        b = Buf(name)
        self.bufs.append(b)
        return b

    def bufs_n(self, name, n):
        return [self.buf(f"{name}{i}") for i in range(n)]

    def op(self, eng, fn, reads=(), writes=(), chan=None):
        o = Op()
        o.eng = eng
        o.fn = fn
        o.sig = False
        o.seq = 0
        o.dma = chan is not None
        o.chan = chan
        o.val = 0
        deps = []
        for b in reads:
            if b.w is not None:
                deps.append((b.w, "raw"))
        for b in writes:
            if b.w is not None:
                deps.append((b.w, "waw"))
            for r in b.r:
                deps.append((r, "war"))
        dd = []
        seen = set()
        for (d, kind) in deps:
            if d is o or id(d) in seen:
                continue
            if (not d.dma) and d.eng == eng:
                if eng == "pe":
                    continue
                if kind != "raw":
                    continue
            seen.add(id(d))
            dd.append(d)
        o.deps = dd
        for b in reads:
            b.r.append(o)
        for b in writes:
            b.w = o
            b.r = []
        self.ops[eng].append(o)
        self.all.append(o)
        return o

    def flush(self, es):
        nc = self.nc
        if not hasattr(self, "cpool"):
            self.cpool = [nc.alloc_semaphore(f"c_{i}") for i in range(26)]
            self.cval = [0] * 26
            self.nflush = 0
        self.nflush += 1
        nf = self.nflush
        sems = {e: nc.alloc_semaphore(f"s_{e}_{nf}") for e in ENGS if e != "sp"}
        for o in self.all:
            for d in o.deps:
                d.sig = True
        chans = []
        for o in self.all:
            if o.dma and o.chan.sem is None:
                i = len(chans)
                o.chan.sem = self.cpool[i]
                o.chan.val = self.cval[i]
                chans.append(o.chan)
        for e in ENGS:
            if e == "sp":
                continue
            for o in reversed(self.ops[e]):
                if not o.dma:
                    o.sig = True
                    break
        cnt = {e: 0 for e in ENGS}
        for e in ENGS:
            for o in self.ops[e]:
                if o.dma:
                    o.chan.val += 16
                    o.val = o.chan.val
                elif o.sig:
                    cnt[e] += 1
                    o.seq = cnt[e]
        final = dict(cnt)
        for i, c in enumerate(chans):
            self.cval[i] = c.val
        block = es.enter_context(nc.Block())

        def run(e, eng):
            waited = {}
            for o in self.ops[e]:
                need = {}
                for d in o.deps:
                    if d.dma:
                        key = ("c", id(d.chan))
                        sem = d.chan.sem
                        v = d.val
                    else:
                        key = ("e", d.eng)
                        sem = sems[d.eng]
                        v = d.seq
                    if waited.get(key, 0) >= v:
                        continue
                    if key not in need or need[key][1] < v:
                        need[key] = (sem, v)
                for key, (sem, v) in need.items():
                    eng.wait_ge(sem, v)
                    waited[key] = v
                ins = o.fn(eng)
                if o.dma:
                    ins.then_inc(o.chan.sem, 16)
                elif o.sig:
                    ins.then_inc(sems[e], 1)
            for f in ENGS:
                if f == "sp":
                    continue
                if final[f] > 0 and waited.get(("e", f), 0) < final[f]:
                    eng.wait_ge(sems[f], final[f])
            for c in chans:
                if waited.get(("c", id(c)), 0) < c.val:
                    eng.wait_ge(c.sem, c.val)

        block.tensor(lambda eng: run("pe", eng))
        block.scalar(lambda eng: run("act", eng))
        block.vector(lambda eng: run("dve", eng))
        block.gpsimd(lambda eng: run("pool", eng))
        block.sync(lambda eng: run("sp", eng))
        for b in self.bufs:
            b.w = None
            b.r = []
            b.sem = None
            b.val = 0
        self.ops = {e: [] for e in ENGS}
        self.all = []


def bc(ap, shape):
    return ap.broadcast_to(list(shape))


def build(n_layers=4, dbg=False):
    nc = bass.Bass("TRN2", target_bir_lowering=False)
    pg = Prog(nc)

    def din(name, shape, dt=F32):
        return nc.dram_tensor(name, list(shape), dt, kind="ExternalInput").ap()

    xT0 = din("xT0", [D, NT])
    cT_d = din("cT", [P, KC * 3])
    normg_d = din("normgT", [P, 4 * KC])
    fing_d = din("fingT", [P, KC])
    adaw_d = din("ada_w", [4, D, 3 * D])
    adab_d = din("adabT", [P, 4 * 24])
    hwin_d = din("hgrn_w_in", [2, D, 5 * E])
    hwout_d = din("hgrn_w_out", [2, E, D])
    lb_d = din("lbT", [P, 2 * 2 * 16])
    hng_d = din("hngT", [P, 2])
    cwin_d = din("conv_w_in", [2, D, 3 * E])
    cwout_d = din("conv_w_out", [2, E, D])
    cbin_d = din("cbinT", [P, 2 * 48])
    cdw_d = din("cdwT", [P, 2 * 16 * 31])
    cdwb_d = din("cdwbT", [P, 2 * 16])
    clng_d = din("clngT", [P, 2 * 16])
    clnb_d = din("clnbT", [P, 2 * 16])
    cbout_d = din("cboutT", [P, 2 * 8])
    ident_d = din("ident", [P, P])
    mask_d = din("masks", [P, 2 * P])
    reset_d = din("resets", [P, 2 * 512])
    outT = nc.dram_tensor("outT", [D, NLAT], F32, kind="ExternalOutput").ap()

    knd = "ExternalOutput" if dbg else "Internal"
    XT = [xT0] + [nc.dram_tensor(f"xT{i}", [D, NT], F32, kind=knd).ap() for i in range(1, 5)]
    OG = nc.dram_tensor("og", [E, NT], BF16, kind=knd).ap()
    UC = nc.dram_tensor("uc", [E, NT], F32, kind=knd).ap()
    HTd = nc.dram_tensor("hTd", [D, NT], BF16, kind=knd).ap()
    XTb = [[pg.buf(f"xt{i}_{t}") for t in range(9)] for i in range(5)]
    OGb = [[pg.buf(f"og{h}_{t}") for t in range(9)] for h in range(16)]
    UCb = [[pg.buf(f"uc{c}_{t}") for t in range(9)] for c in range(16)]
    HTdb = [pg.buf(f"htd{t}") for t in range(9)]

    top = ExitStack()

    uniq = [0]

    def sb(es, name, shape, dt=F32):
        uniq[0] += 1
        return es.enter_context(nc.sbuf_tensor(f"{name}_{uniq[0]}", list(shape), dt))

    def ps(es, name, shape, dt=F32):
        uniq[0] += 1
        return es.enter_context(nc.psum_tensor(f"{name}_{uniq[0]}", list(shape), dt))

    hT = sb(top, "hT", [P, KC, NT], BF16)
    hTb = [pg.buf(f"hT{t}") for t in range(9)]
    MOD = sb(top, "MOD", [P, 4, 3, KC, 3])
    MODb = pg.buf("MOD")
    BG = sb(top, "BG", [P, 4, KC, 3])
    LBA = sb(top, "LBA", [P, 2, 2, 16])
    LBB = sb(top, "LBB", [P, 2, 2, 16])
    LBN = sb(top, "LBN", [P, 2, 2, 16])
    normg = sb(top, "normg", [P, 4, KC])
    fing = sb(top, "fing", [P, KC])
    hng = sb(top, "hng", [P, 2])
    cbin = sb(top, "cbin", [P, 2, 48])
    cdw = sb(top, "cdw", [P, 2, 16, 31])
    cdwb = sb(top, "cdwb", [P, 2, 16])
    clng = sb(top, "clng", [P, 2, 16])
    clnb = sb(top, "clnb", [P, 2, 16])
    cbout = sb(top, "cbout", [P, 2, 8])
    ident = sb(top, "identf", [P, P])
    identb = sb(top, "identb", [P, P], BF16)
    onesb = sb(top, "onesb", [P, P], BF16)
    onesf = sb(top, "onesf", [P, P])
    masks = sb(top, "masksb", [P, 2, P])
    resets = sb(top, "resetsb", [P, 2, 512])
    zcol = sb(top, "zcol", [P, 1])
    CONSTb = pg.buf("consts")

    with ExitStack() as es:
        cT = sb(es, "cTs", [P, KC, 3])
        th = sb(es, "cth", [P, KC, 3])
        scT = sb(es, "scT", [P, KC, 3])
        adab = sb(es, "adab", [P, 4, 24])
        lbl = sb(es, "lbl", [P, 2, 2, 16])
        lbt = sb(es, "lbt", [P, 2, 16])
        adaT = sb(es, "adaT", [P, 4, 24, 3])
        aw = [sb(es, f"aw{i}", [P, KC, 768]) for i in range(2)]
        awb = pg.bufs_n("aw", 2)
        pada = [ps(es, f"pada{i}", [P, 512]) for i in range(4)]
        padab = pg.bufs_n("pada", 4)
        smallb = pg.buf("small")
        scb = pg.buf("scT")
        adaTb = pg.buf("adaT")

        loads = [(cT[:].rearrange("p a b -> p (a b)"), cT_d), (normg[:].rearrange("p a b -> p (a b)"), normg_d),
                 (fing[:], fing_d), (adab[:].rearrange("p a b -> p (a b)"), adab_d),
                 (lbl[:].rearrange("p a b c -> p (a b c)"), lb_d), (hng[:], hng_d),
                 (cbin[:].rearrange("p a b -> p (a b)"), cbin_d), (cdw[:].rearrange("p a b c -> p (a b c)"), cdw_d),
                 (cdwb[:].rearrange("p a b -> p (a b)"), cdwb_d), (clng[:].rearrange("p a b -> p (a b)"), clng_d),
                 (clnb[:].rearrange("p a b -> p (a b)"), clnb_d), (cbout[:].rearrange("p a b -> p (a b)"), cbout_d),
                 (ident[:], ident_d), (masks[:].rearrange("p a b -> p (a b)"), mask_d),
                 (resets[:].rearrange("p a b -> p (a b)"), reset_d)]
        for i, (dst, src) in enumerate(loads):
            b = pg.buf(f"ld{i}")
            pg.op("sp", lambda e, dst=dst, src=src: e.dma_start(out=dst, in_=src[:, :]), writes=[b], chan=b)

        def consts(e):
            e.tensor_copy(out=identb[:], in_=ident[:])
            e.memset(onesb[:], 1.0)
            e.memset(onesf[:], 1.0)
            e.memset(zcol[:], 0.0)
            return e.memset(BG[:].rearrange("p a b c -> p (a b c)"), 0.0)
        all_ld = [b for b in pg.bufs if b.name.startswith("ld")]
        pg.op("dve", consts, reads=all_ld, writes=[CONSTb])
        pg.op("act", lambda e: e.activation(out=th[:], in_=cT[:], func=AF.Tanh, scale=0.5), reads=all_ld, writes=[scb])

        pg.op("dve", lambda e: e.tensor_scalar(out=th[:], in0=th[:], scalar1=0.5, scalar2=0.5, op0=ALU.mult, op1=ALU.add), reads=[scb, CONSTb], writes=[scb])
        pg.op("dve", lambda e: e.tensor_tensor(out=scT[:], in0=th[:], in1=cT[:], op=ALU.mult), reads=[scb], writes=[scb])
        def lb1(e):
            return e.tensor_tensor(out=lbt[:], in0=lbl[:, :, 1, :], in1=lbl[:, :, 0, :], op=ALU.subtract)
        lbb = pg.buf("lbb")
        pg.op("dve", lb1, reads=all_ld + [scb], writes=[lbb])
        pg.op("act", lambda e: e.activation(out=lbt[:], in_=lbt[:], func=AF.Tanh, scale=0.5), reads=[lbb], writes=[lbb])

        def lb2(e):
            e.memset(LBA[:, 0, :, :], 0.5)
            e.memset(LBB[:, 0, :, :], 0.5)
            e.memset(LBN[:, 0, :, :], -0.5)
            e.tensor_scalar(out=LBA[:, 1, :, :], in0=lbt[:], scalar1=0.25, scalar2=0.75, op0=ALU.mult, op1=ALU.add)
            e.tensor_scalar(out=LBB[:, 1, :, :], in0=lbt[:], scalar1=-0.25, scalar2=0.25, op0=ALU.mult, op1=ALU.add)
            return e.tensor_scalar(out=LBN[:, 1, :, :], in0=lbt[:], scalar1=0.25, scalar2=-0.25, op0=ALU.mult, op1=ALU.add)
        pg.op("dve", lb2, reads=[lbb], writes=[CONSTb])

        for l in range(4):
            for pc in range(4):
                i = (l * 4 + pc) % 2
                src = adaw_d[l].rearrange("(kc p) n -> p kc n", p=P)[:, :, pc * 768:(pc + 1) * 768]
                pg.op("sp", lambda e, i=i, src=src: e.dma_start(out=aw[i][:], in_=src), writes=[awb[i]], chan=awb[i])

                def mm(e, l=l, pc=pc, i=i):
                    ins = None
                    for oc in range(6):
                        o = pc * 6 + oc
                        for kc in range(KC):
                            ins = e.matmul(pada[l][:, o * 3:(o + 1) * 3], lhsT=aw[i][:, kc, oc * 128:(oc + 1) * 128],
                                           rhs=scT[:, kc, :], start=(kc == 0), stop=(kc == KC - 1))
                    return ins
                pg.op("pe", mm, reads=[awb[i], scb], writes=[padab[l]])

            def ev(e, l=l):
                return e.tensor_tensor(out=adaT[:, l, :, :], in0=pada[l][:, 0:72].rearrange("p (a b) -> p a b", b=3),
                                       in1=bc(adab[:, l, :].unsqueeze(2), [P, 24, 3]), op=ALU.add)
            pg.op("dve", ev, reads=[padab[l]] + all_ld, writes=[adaTb])

            def mod(e, l=l):
                e.tensor_copy(out=MOD[:, l, 0, :, :], in_=adaT[:, l, 0:8, :])
                e.tensor_copy(out=MOD[:, l, 2, :, :], in_=adaT[:, l, 16:24, :])
                return e.scalar_tensor_tensor(out=MOD[:, l, 1, :, :], in0=adaT[:, l, 8:16, :], scalar=1.0,
                                              in1=bc(normg[:, l, :].unsqueeze(2), [P, KC, 3]), op0=ALU.add, op1=ALU.mult)
            pg.op("dve", mod, reads=[adaTb] + all_ld, writes=[MODb])
            if l % 2 == 1:
                def bgf(e, l=l):
                    return e.tensor_tensor(out=BG[:, l, :, :], in0=adaT[:, l, 16:24, :],
                                           in1=bc(cbout[:, l // 2, :].unsqueeze(2), [P, KC, 3]), op=ALU.mult)
                pg.op("pool", bgf, reads=[adaTb, CONSTb] + all_ld, writes=[MODb])
        pg.flush(es)

    def tile_cols(t):
        return t * 512, 512

    def tile_j(t):
        return 2 if t == 8 else t // 4

    def xt_view(x_ap, t):
        c0, n = tile_cols(t)
        return x_ap.rearrange("(kc p) n -> p kc n", p=P)[:, :, c0:c0 + n]

    def phase_norm(l, src_i, tiles, final=False):
        with ExitStack() as es:
            xt = [sb(es, f"xt{i}", [P, KC, 512]) for i in range(2)]
            xtb = pg.bufs_n("nxt", 2)
            sq = sb(es, "sq", [P, KC, 512], BF16)
            sqb = pg.buf("sq")
            lnv = sb(es, "lnv", [P, 512])
            rstd = sb(es, "rstd", [P, 512])
            rsb = pg.buf("rstd")
            tmp = sb(es, "tmp", [P, KC, 512])
            tmpb = pg.buf("tmp")
            ot = [sb(es, f"ot{i}", [P, KC, 512]) for i in range(2)] if final else None
            otb = pg.bufs_n("ot", 2)
            pss = [ps(es, f"pss{i}", [P, 512]) for i in range(2)]
            pssb = pg.bufs_n("pss", 2)
            def loadn(n):
                t = tiles[n]
                i = n % 2
                pg.op("sp", lambda e, i=i, t=t: e.dma_start(out=xt[i][:], in_=xt_view(XT[src_i], t)),
                      reads=[XTb[src_i][t]], writes=[xtb[i]], chan=xtb[i])
            loadn(0)
            for n, t in enumerate(tiles):
                i = n % 2
                c0, ncol = tile_cols(t)
                j = tile_j(t)
                if n + 1 < len(tiles):
                    loadn(n + 1)
                pg.op("act", lambda e, i=i: e.activation(out=sq[:], in_=xt[i][:], func=AF.Square), reads=[xtb[i]], writes=[sqb])

                def mm(e, i=i):
                    ins = None
                    for kc in range(KC):
                        ins = e.matmul(pss[i][:], lhsT=onesb[:], rhs=sq[:, kc, :], start=(kc == 0), stop=(kc == KC - 1))
                    return ins
                pg.op("pe", mm, reads=[sqb, CONSTb], writes=[pssb[i]])
                pg.op("act", lambda e, i=i: e.activation(out=lnv[:], in_=pss[i][:], func=AF.Ln, scale=1.0 / D, bias=EPS),
                      reads=[pssb[i]], writes=[rsb])
                pg.op("act", lambda e: e.activation(out=rstd[:], in_=lnv[:], func=AF.Exp, scale=-0.5), reads=[rsb], writes=[rsb])
                pg.op("dve", lambda e, i=i: e.tensor_tensor(out=tmp[:], in0=xt[i][:], in1=bc(rstd[:].unsqueeze(1), [P, KC, 512]), op=ALU.mult),
                      reads=[xtb[i], rsb], writes=[tmpb])
                if not final:
                    def aff(e, c0=c0, j=j):
                        ins = None
                        for kc in range(KC):
                            ins = e.tensor_scalar(out=hT[:, kc, c0:c0 + 512], in0=tmp[:, kc, :], scalar1=MOD[:, l, 1, kc, j:j + 1],
                                                  scalar2=MOD[:, l, 0, kc, j:j + 1], op0=ALU.mult, op1=ALU.add)
                        return ins
                    pg.op("pool", aff, reads=[tmpb, MODb], writes=[hTb[t]])
                    if l % 2 == 1 or dbg:
                        pg.op("sp", lambda e, c0=c0: e.dma_start(out=HTd.rearrange("(kc p) n -> p kc n", p=P)[:, :, c0:c0 + 512],
                                                                   in_=hT[:, :, c0:c0 + 512]),
                              reads=[hTb[t]], writes=[HTdb[t]], chan=hTb[t])
                else:
                    def aff(e, i=i):
                        return e.tensor_tensor(out=ot[i][:], in0=tmp[:], in1=bc(fing[:].unsqueeze(2), [P, KC, 512]), op=ALU.mult)
                    pg.op("pool", aff, reads=[tmpb, CONSTb], writes=[otb[i]])
                    pg.op("sp", lambda e, i=i, c0=c0: e.dma_start(out=outT.rearrange("(kc p) n -> p kc n", p=P)[:, :, c0:c0 + 512], in_=ot[i][:]),
                          reads=[otb[i]], writes=[], chan=otb[i])
            pg.flush(es)

    def phase_out(l, w_d, src_b, tiles):
        with ExitStack() as es:
            w = sb(es, "wout", [P, EC, D], BF16)
            wb = pg.bufs_n("wout", 8)
            og = [sb(es, f"ogt{i}", [P, EC, 512], BF16) for i in range(2)]
            ogb = pg.bufs_n("ogt", 2)
            xt = [sb(es, f"oxt{i}", [P, KC, 512]) for i in range(2)]
            xtb = pg.bufs_n("oxt", 2)
            xn = xt
            xnb = xtb
            yt = sb(es, "yt", [P, 2, 512])
            ytb = pg.bufs_n("yt", 2)
            py = [ps(es, f"py{i}", [P, 512]) for i in range(4)]
            pyb = pg.bufs_n("py", 4)
            for q in range(8):
                src = w_d.rearrange("(ec p) n -> p ec n", p=P)[:, :, q * 128:(q + 1) * 128]
                pg.op("pool", lambda e, q=q, src=src: e.dma_start(out=w[:, :, q * 128:(q + 1) * 128], in_=src), writes=[wb[q]], chan=wb[q])
            cnt = 0

            def loads(n):
                t = tiles[n]
                i = n % 2
                c0, ncol = tile_cols(t)
                pg.op("sp", lambda e, i=i, c0=c0: e.dma_start(out=og[i][:], in_=OG.rearrange("(ec p) n -> p ec n", p=P)[:, :, c0:c0 + 512]),
                      reads=[bb[t] for bb in src_b], writes=[ogb[i]], chan=ogb[i])
                pg.op("sp", lambda e, i=i, t=t: e.dma_start(out=xt[i][:], in_=xt_view(XT[l], t)),
                      reads=[XTb[l][t]], writes=[xtb[i]], chan=xtb[i])
            loads(0)
            for n, t in enumerate(tiles):
                i = n % 2
                c0, ncol = tile_cols(t)
                j = tile_j(t)
                if n + 1 < len(tiles):
                    loads(n + 1)
                for dc in range(KC):
                    k = cnt % 4
                    k2 = cnt % 2
                    cnt += 1

                    def mm(e, i=i, dc=dc, k=k):
                        ins = None
                        for ec in range(EC):
                            ins = e.matmul(py[k][:], lhsT=w[:, ec, dc * 128:(dc + 1) * 128], rhs=og[i][:, ec, :],
                                           start=(ec == 0), stop=(ec == EC - 1))
                        return ins
                    pg.op("pe", mm, reads=[ogb[i], wb[dc]], writes=[pyb[k]])
                    pg.op("act", lambda e, k=k, k2=k2, dc=dc, j=j: e.activation(out=yt[:, k2, :], in_=py[k][:], func=AF.Identity,
                                                                           scale=MOD[:, l, 2, dc, j:j + 1], bias=BG[:, l, dc, j:j + 1]),
                          reads=[pyb[k], MODb], writes=[ytb[k2]])
                    pg.op("dve", lambda e, i=i, k2=k2, dc=dc: e.tensor_tensor(out=xn[i][:, dc, :], in0=yt[:, k2, :], in1=xt[i][:, dc, :], op=ALU.add),
                          reads=[ytb[k2], xtb[i]], writes=[xnb[i]])
                pg.op("sp", lambda e, i=i, t=t: e.dma_start(out=xt_view(XT[l + 1], t), in_=xn[i][:]),
                      reads=[xnb[i]], writes=[XTb[l + 1][t]], chan=xnb[i])
            pg.flush(es)

    TB = 64
    HB = 32
    NBK = 36
    U32 = mybir.dt.uint32

    def phase_hgrn(l):
        jl = l // 2
        with ExitStack() as es:
            Wset = [sb(es, f"W{i}", [P, 5, KC, P], BF16) for i in range(2)]
            Wbset = [pg.bufs_n(f"W{i}_", 5) for i in range(2)]
            QS = sb(es, "QS", [P, 2304])
            GT = sb(es, "GT", [P, 2304], BF16)
            OACC = sb(es, "OACC", [P, 2304])
            VT = sb(es, "VT", [TB, NBK, P], BF16)
            QT = [sb(es, f"QT{d}", [P, 2304], BF16) for d in range(2)]
            KT = [sb(es, f"KT{d}", [P, 2304], BF16) for d in range(2)]
            QH = [sb(es, f"QH{d}", [P, 2304], BF16) for d in range(2)]
            KH = [sb(es, f"KH{d}", [P, 2304], BF16) for d in range(2)]
            KCc = [sb(es, f"KC{d}", [P, NBK, HB], BF16) for d in range(2)]
            BB = [sb(es, f"BB{d}", [P, 512]) for d in range(2)]
            CQ = [sb(es, f"CQ{d}", [P, 2 * NBK]) for d in range(2)]
            CK = [sb(es, f"CK{d}", [P, 2 * NBK]) for d in range(2)]
            CC = [sb(es, f"CC{d}", [P, NBK]) for d in range(2)]
            DE = [sb(es, f"DE{d}", [P, NBK]) for d in range(2)]
            CT1 = [sb(es, f"CT1{d}", [P, 2 * NBK]) for d in range(2)]
            CT2 = [sb(es, f"CT2{d}", [P, NBK]) for d in range(2)]
            S32 = sb(es, "S32", [P, 2, P])
            SB = sb(es, "SB", [P, 2, 2, P], BF16)
            AT = sb(es, "AT", [TB, 3, 2, TB], BF16)
            KHT = sb(es, "KHT", [TB, 3, 2, P], BF16)
            TQ = sb(es, "TQ", [P, 512])
            TG = sb(es, "TG", [P, 512])
            TT = [sb(es, f"TTs{d}", [P, 512]) for d in range(2)]
            GG = [sb(es, "GGs", [P, 512])] * 2
            KK = [sb(es, "KKs", [P, 512])] * 2
            DDt = [sb(es, "DDs", [P, 512])] * 2
            EQ = [sb(es, "EQs", [P, 512])] * 2
            EK = DDt
            SQ = sb(es, "SQo", [P, 512], BF16)
            LNV = TQ
            RS = TG
            T1 = TT[0]
            OGt = [sb(es, "OGo", [P, 512], BF16)] * 2
            hng2 = sb(es, "hng2", [P, 1])
            pq = ps(es, "pq", [P, 512]); pgt = ps(es, "pgt", [P, 512]); pz = [ps(es, f"pz{d}", [P, 512]) for d in range(2)]
            pv = ps(es, "pv", [P, 512])
            pA = ps(es, "pA", [P, 3, 2, TB])
            pO = ps(es, "pO", [P, 2, 2, TB])
            pT = ps(es, "pT", [P, 3, 2, P], BF16)
            b_ = pg.buf
            QSb, GTb, VTb = b_("QS"), b_("GT"), b_("VT")
            OACCb = pg.bufs_n("OACC", NBK)
            QTb, KTb, QHb, KHb, KCb, BBb = (pg.bufs_n(n, 2) for n in ("QT", "KT", "QH", "KH", "KC", "BB"))
            CSTb = pg.bufs_n("CST", 2)
            S32b, SBb = (pg.bufs_n(n, 2) for n in ("S32", "SB"))
            ATb, KHTb = (pg.bufs_n(n, 3) for n in ("AT", "KHT"))
            TQb, TGb = b_("TQ"), b_("TG")
            GGb, KKb, DDb, EQb = ([pg.buf(n)] * 2 for n in ("GG", "KK", "DD", "EQ"))
            EKb = DDb
            TTb = pg.bufs_n("TT", 2)
            SQb = b_("SQo")
            OGtb = [pg.buf("OGo")] * 2
            pqb, pgtb, pvb = b_("pq"), b_("pgt"), b_("pv")
            pPb = [pg.bufs_n(f"pP{d}_", 4) for d in range(2)]
            pAb, pTb = (pg.bufs_n(n, 3) for n in ("pA", "pT"))
            pOb = pg.bufs_n("pO", 2)
            zb = b_("zero")

            def init(e):
                e.memset(AT[:].rearrange("p a b c -> p (a b c)"), 0.0)
                return e.tensor_scalar(out=hng2[:], in0=hng[:, jl:jl + 1], scalar1=0.5, scalar2=None, op0=ALU.mult)
            pg.op("dve", init, reads=[CONSTb], writes=ATb + [zb])
            ogcnt = 0
            def load_w(hh):
                for s in range(5):
                    col0 = s * E + hh * P
                    src = hwin_d[jl].rearrange("(kc p) n -> p kc n", p=P)[:, :, col0:col0 + P]
                    pg.op("pool", lambda e, s=s, src=src, hh=hh: e.dma_start(out=Wset[hh % 2][:, s, :, :], in_=src), writes=[Wbset[hh % 2][s]], chan=Wbset[hh % 2][s])
            load_w(0)
            for h in range(16):
                W = Wset[h % 2]
                Wb = Wbset[h % 2]
                Acol = [LBA[:, jl, d, h:h + 1] for d in range(2)]
                Bcol = [LBB[:, jl, d, h:h + 1] for d in range(2)]
                Ncol = [LBN[:, jl, d, h:h + 1] for d in range(2)]
                for b in range(2):
                    if b == 1 and h + 1 < 16:
                        load_w(h + 1)
                    ltiles = [(0, 256, NLAT + b * 256, 8)] + [(256 + i * 512, 512, b * 2048 + i * 512, b * 4 + i) for i in range(4)]
                    for (lc, n, gc, gt) in ltiles:
                        rd = [hTb[gt]]

                        def proj(e, s, dst, n=n, gc=gc, W=W):
                            ins = None
                            for kc in range(KC):
                                ins = e.matmul(dst[:, 0:n], lhsT=W[:, s, kc, :], rhs=hT[:, kc, gc:gc + n], start=(kc == 0), stop=(kc == KC - 1))
                            return ins
                        pg.op("pe", lambda e, f=proj: f(e, 0, pq), reads=rd + [Wb[0]], writes=[pqb])
                        pg.op("pe", lambda e, f=proj: f(e, 4, pgt), reads=rd + [Wb[4]], writes=[pgtb])
                        pg.op("pe", lambda e, f=proj: f(e, 2, pz[0]), reads=rd + [Wb[2]], writes=pPb[0])
                        pg.op("pe", lambda e, f=proj: f(e, 3, pz[1]), reads=rd + [Wb[3]], writes=pPb[1])
                        for v0 in range(0, n // TB, 4):
                            def projv(e, gc=gc, v0=v0, W=W):
                                ins = None
                                for bi in range(4):
                                    t0 = gc + (v0 + bi) * TB
                                    for kc in range(KC):
                                        ins = e.matmul(pv[0:TB, bi * P:(bi + 1) * P], lhsT=hT[:, kc, t0:t0 + TB], rhs=W[:, 1, kc, :],
                                                       start=(kc == 0), stop=(kc == KC - 1))
                                return ins
                            pg.op("pe", projv, reads=rd + [Wb[1]], writes=[pvb])
                            bl0 = lc // TB + v0
                            pg.op("dve", lambda e, bl0=bl0: e.tensor_copy(out=VT[:, bl0:bl0 + 4, :], in_=pv[0:TB, :].rearrange("p (a b) -> p a b", b=P)),
                                  reads=[pvb], writes=[VTb])
                        pg.op("act", lambda e, n=n: e.activation(out=TQ[:, 0:n], in_=pq[:, 0:n], func=AF.Tanh, scale=0.5), reads=[pqb], writes=[TQb])
                        pg.op("act", lambda e, n=n: e.activation(out=TG[:, 0:n], in_=pgt[:, 0:n], func=AF.Tanh, scale=0.5), reads=[pgtb], writes=[TGb])
                        for d in range(2):
                            pg.op("act", lambda e, n=n, d=d: e.activation(out=TT[d][:, 0:n], in_=pz[d][:, 0:n], func=AF.Tanh, scale=0.5),
                                  reads=pPb[d], writes=[TTb[d]])
                        pg.op("dve", lambda e, n=n, lc=lc: e.scalar_tensor_tensor(out=QS[:, lc:lc + n], in0=TQ[:, 0:n], scalar=1.0, in1=pq[:, 0:n], op0=ALU.add, op1=ALU.mult),
                              reads=[TQb, pqb], writes=[QSb])
                        pg.op("dve", lambda e, n=n, lc=lc: e.scalar_tensor_tensor(out=GT[:, lc:lc + n], in0=TG[:, 0:n], scalar=1.0, in1=pgt[:, 0:n], op0=ALU.add, op1=ALU.mult),
                              reads=[TGb, pgtb], writes=[GTb])
                        for d in range(2):
                            nh = n // HB
                            nb = n // TB
                            h0 = lc // HB
                            b0 = lc // TB
                            rpos = 15 if d == 0 else 16
                            pg.op("act", lambda e, n=n, d=d: e.activation(out=GG[d][:, 0:n], in_=TT[d][:, 0:n], func=AF.Ln, scale=Bcol[d], bias=Acol[d]),
                                  reads=[TTb[d], CONSTb], writes=[GGb[d]])
                            pg.op("pool", lambda e, n=n, d=d: e.tensor_scalar(out=KK[d][:, 0:n], in0=TT[d][:, 0:n], scalar1=Ncol[d], scalar2=Bcol[d], op0=ALU.mult, op1=ALU.add),
                                  reads=[TTb[d], CONSTb], writes=[KKb[d]])

                            def scan(e, n=n, d=d):
                                if d == 0:
                                    return e.tensor_tensor_scan(out=BB[d][:, 0:n], data0=resets[:, 0, 0:n], data1=GG[d][:, 0:n], initial=0.0, op0=ALU.mult, op1=ALU.add)
                                return e.tensor_tensor_scan(out=BB[d][:, 0:n][:, ::-1], data0=resets[:, 1, 0:n][:, ::-1], data1=GG[d][:, 0:n][:, ::-1],
                                                            initial=0.0, op0=ALU.mult, op1=ALU.add)
                            pg.op("dve", scan, reads=[GGb[d], CONSTb], writes=[BBb[d]])
                            BBv = BB[d][:, 0:n].rearrange("p (a b) -> p a b", b=HB)

                            def dd(e, n=n, d=d, BBv=BBv, nh=nh, rpos=rpos):
                                return e.tensor_tensor(out=DDt[d][:, 0:n].rearrange("p (a b) -> p a b", b=HB), in0=BBv,
                                                       in1=bc(BBv[:, :, rpos:rpos + 1], [P, nh, HB]), op=ALU.subtract)
                            pg.op("dve", dd, reads=[BBb[d]], writes=[DDb[d]])
                            pg.op("act", lambda e, n=n, d=d: e.activation(out=EQ[d][:, 0:n], in_=DDt[d][:, 0:n], func=AF.Exp), reads=[DDb[d]], writes=[EQb[d]])
                            pg.op("act", lambda e, n=n, d=d: e.activation(out=EK[d][:, 0:n], in_=DDt[d][:, 0:n], func=AF.Exp, scale=-1.0), reads=[DDb[d]], writes=[EKb[d]])
                            pg.op("dve", lambda e, n=n, d=d, lc=lc: e.scalar_tensor_tensor(out=QT[d][:, lc:lc + n], in0=QS[:, lc:lc + n], scalar=0.5, in1=EQ[d][:, 0:n], op0=ALU.mult, op1=ALU.mult),
                                  reads=[QSb, EQb[d]], writes=[QTb[d]])
                            pg.op("pool", lambda e, n=n, d=d, lc=lc: e.tensor_tensor(out=KT[d][:, lc:lc + n], in0=KK[d][:, 0:n], in1=EK[d][:, 0:n], op=ALU.mult),
                                  reads=[KKb[d], EKb[d]], writes=[KTb[d]])
                            BB4 = BB[d][:, 0:n].rearrange("p (a h b) -> p a h b", h=2, b=HB)
                            fh, sh = (0, 1) if d == 0 else (1, 0)
                            eh, ep = (1, HB - 1) if d == 0 else (0, 0)

                            def cst(e, d=d, BB4=BB4, nb=nb, nh=nh, h0=h0, b0=b0, rpos=rpos, fh=fh, sh=sh, eh=eh, ep=ep):
                                e.tensor_tensor(out=CT1[d][:, h0:h0 + nh].rearrange("p (a h) -> p a h", h=2),
                                                in0=bc(BB4[:, :, eh, ep:ep + 1], [P, nb, 2]), in1=BB4[:, :, :, rpos], op=ALU.subtract)
                                return e.tensor_tensor(out=CT2[d][:, b0:b0 + nb], in0=BB4[:, :, sh, rpos], in1=BB4[:, :, fh, rpos], op=ALU.subtract)
                            pg.op("dve", cst, reads=[BBb[d]], writes=[CSTb[d]])

                            def cst2(e, d=d, BB4=BB4, nb=nb, nh=nh, h0=h0, b0=b0, rpos=rpos, eh=eh, ep=ep):
                                e.activation(out=CQ[d][:, h0:h0 + nh].rearrange("p (a h) -> p a h", h=2), in_=BB4[:, :, :, rpos], func=AF.Exp)
                                e.activation(out=CK[d][:, h0:h0 + nh], in_=CT1[d][:, h0:h0 + nh], func=AF.Exp)
                                e.activation(out=CC[d][:, b0:b0 + nb], in_=CT2[d][:, b0:b0 + nb], func=AF.Exp)
                                return e.activation(out=DE[d][:, b0:b0 + nb], in_=BB4[:, :, eh, ep], func=AF.Exp)
                            pg.op("act", cst2, reads=[BBb[d], CSTb[d]], writes=[CSTb[d]])
                            pg.op("pool", lambda e, n=n, d=d, lc=lc, nh=nh, h0=h0: e.tensor_tensor(
                                out=QH[d][:, lc:lc + n].rearrange("p (a b) -> p a b", b=HB), in0=QT[d][:, lc:lc + n].rearrange("p (a b) -> p a b", b=HB),
                                in1=bc(CQ[d][:, h0:h0 + nh].unsqueeze(2), [P, nh, HB]), op=ALU.mult), reads=[QTb[d], CSTb[d]], writes=[QHb[d]])
                            pg.op("pool", lambda e, n=n, d=d, lc=lc, nh=nh, h0=h0: e.tensor_tensor(
                                out=KH[d][:, lc:lc + n].rearrange("p (a b) -> p a b", b=HB), in0=KT[d][:, lc:lc + n].rearrange("p (a b) -> p a b", b=HB),
                                in1=bc(CK[d][:, h0:h0 + nh].unsqueeze(2), [P, nh, HB]), op=ALU.mult), reads=[KTb[d], CSTb[d]], writes=[KHb[d]])
                            pg.op("pool", lambda e, n=n, d=d, lc=lc, nb=nb, b0=b0, fh=fh: e.tensor_tensor(
                                out=KCc[d][:, b0:b0 + nb, :], in0=KT[d][:, lc:lc + n].rearrange("p (a h b) -> p a h b", h=2, b=HB)[:, :, fh, :],
                                in1=bc(CC[d][:, b0:b0 + nb].unsqueeze(2), [P, nb, HB]), op=ALU.mult), reads=[KTb[d], CSTb[d]], writes=[KCb[d]])
                    pg.op("dve", lambda e: e.memset(S32[:].rearrange("p a b -> p (a b)"), 0.0), writes=S32b)
                    pg.op("pool", lambda e: e.memset(SB[:, 0, :, :], 0.0), writes=[SBb[0]])
                    order = [list(range(NBK)), [3, 2, 1, 0] + list(range(NBK - 1, 3, -1))]
                    seen = set()
                    F0s = (0, HB)
                    S0s = (HB, 0)

                    def need_o(step):
                        return True

                    def stage_a1(step):
                        par = step % 3
                        blks = [order[d][step] for d in range(2)]
                        if need_o(step):
                            def sc(e, par=par, blks=blks):
                                ins = None
                                for d in range(2):
                                    c0 = blks[d] * TB
                                    F0, S0 = F0s[d], S0s[d]
                                    e.matmul(pA[F0:F0 + HB, par, d, F0:F0 + HB], lhsT=KT[d][:, c0 + F0:c0 + F0 + HB], rhs=QT[d][:, c0 + F0:c0 + F0 + HB], start=True, stop=True)
                                    e.matmul(pA[S0:S0 + HB, par, d, S0:S0 + HB], lhsT=KT[d][:, c0 + S0:c0 + S0 + HB], rhs=QT[d][:, c0 + S0:c0 + S0 + HB], start=True, stop=True)
                                    ins = e.matmul(pA[F0:F0 + HB, par, d, S0:S0 + HB], lhsT=KCc[d][:, blks[d], :], rhs=QT[d][:, c0 + S0:c0 + S0 + HB], start=True, stop=True)
                                return ins
                            pg.op("pe", sc, reads=KTb + QTb + KCb + [zb], writes=[pAb[par]])
                            pg.op("dve", lambda e, par=par: e.copy_predicated(out=AT[:, par, :, :], mask=masks[0:TB, :, 0:TB].bitcast(U32), data=pA[0:TB, par, :, :]),
                                  reads=[pAb[par], CONSTb, zb], writes=[ATb[par]])
                        if step < NBK - 1:
                            def tr(e, par=par, blks=blks):
                                ins = None
                                for d in range(2):
                                    c0 = blks[d] * TB
                                    ins = e.transpose(out=pT[0:TB, par, d, :], in_=KH[d][:, c0:c0 + TB], identity=identb[:])
                                return ins
                            pg.op("pe", tr, reads=KHb + [CONSTb], writes=[pTb[par]])
                            pg.op("act", lambda e, par=par: e.activation(out=KHT[:, par, :, :], in_=pT[0:TB, par, :, :], func=AF.Copy), reads=[pTb[par]], writes=[KHTb[par]])

                    def stage_a2(step):
                        if step >= NBK - 1:
                            return
                        par = step % 3
                        k = step % 4
                        blks = [order[d][step] for d in range(2)]

                        def pm(e, par=par, k=k, blks=blks):
                            ins = None
                            for d in range(2):
                                ins = e.matmul(pz[d][:, k * P:(k + 1) * P], lhsT=KHT[:, par, d, :], rhs=VT[:, blks[d], :], start=True, stop=True)
                            return ins
                        pg.op("pe", pm, reads=[KHTb[par], VTb], writes=[pPb[0][k], pPb[1][k]])

                    def stage_om(step):
                        if not need_o(step):
                            return
                        par = step % 2
                        blks = [order[d][step] for d in range(2)]

                        def om(e, par=par, blks=blks, step=step):
                            ins = None
                            for d in range(2):
                                c0 = blks[d] * TB
                                e.matmul(pO[:, par, d, :], lhsT=SB[:, par, d, :], rhs=QH[d][:, c0:c0 + TB], start=True, stop=False)
                                ins = e.matmul(pO[:, par, d, :], lhsT=VT[:, blks[d], :], rhs=AT[:, step % 3, d, :], start=False, stop=True)
                            return ins
                        pg.op("pe", om, reads=[SBb[par], ATb[step % 3], VTb] + QHb, writes=[pOb[par]])

                    def stage_upd(step):
                        if step >= NBK - 1:
                            return
                        k = step % 4
                        blks = [order[d][step] for d in range(2)]
                        for d in range(2):
                            pg.op("dve", lambda e, d=d, k=k, blk=blks[d]: e.scalar_tensor_tensor(out=S32[:, d, :], in0=S32[:, d, :], scalar=DE[d][:, blk:blk + 1], in1=pz[d][:, k * P:(k + 1) * P], op0=ALU.mult, op1=ALU.add),
                                  reads=[S32b[d], CSTb[d], pPb[d][k]], writes=[S32b[d]])
                        npar = (step + 1) % 2
                        pg.op("pool", lambda e, npar=npar: e.tensor_copy(out=SB[:, npar, :, :], in_=S32[:]), reads=S32b, writes=[SBb[npar]])

                    def stage_evac(step):
                        if not need_o(step):
                            return
                        par = step % 2
                        for d in range(2):
                            blk = order[d][step]
                            c0 = blk * TB
                            if blk not in seen:
                                seen.add(blk)
                                pg.op("act", lambda e, c0=c0, d=d, par=par: e.activation(out=OACC[:, c0:c0 + TB], in_=pO[:, par, d, :], func=AF.Copy),
                                      reads=[pOb[par]], writes=[OACCb[blk]])
                            else:
                                pg.op("dve", lambda e, c0=c0, d=d, par=par: e.tensor_tensor(out=OACC[:, c0:c0 + TB], in0=pO[:, par, d, :], in1=OACC[:, c0:c0 + TB], op=ALU.add),
                                      reads=[pOb[par], OACCb[blk]], writes=[OACCb[blk]])

                    stage_a1(0)
                    stage_a2(0)
                    stage_a1(1)
                    stage_a2(1)
                    for step in range(NBK):
                        if step + 2 < NBK:
                            stage_a1(step + 2)
                        stage_om(step)
                        if step + 2 < NBK:
                            stage_a2(step + 2)
                        stage_upd(step)
                        stage_evac(step)
                    for (lc, n, gc, gt) in ltiles:
                        i = ogcnt % 2
                        ogcnt += 1
                        pg.op("act", lambda e, n=n, lc=lc: e.activation(out=SQ[:, 0:n], in_=OACC[:, lc:lc + n], func=AF.Square), reads=OACCb[lc // TB:(lc + n) // TB], writes=[SQb])
                        pg.op("pe", lambda e, n=n: e.matmul(pq[:, 0:n], lhsT=onesb[:], rhs=SQ[:, 0:n], start=True, stop=True),
                              reads=[SQb, CONSTb], writes=[pqb])
                        pg.op("act", lambda e, n=n: e.activation(out=LNV[:, 0:n], in_=pq[:, 0:n], func=AF.Ln, scale=1.0 / P, bias=EPS), reads=[pqb], writes=[TQb])
                        pg.op("act", lambda e, n=n: e.activation(out=RS[:, 0:n], in_=LNV[:, 0:n], func=AF.Exp, scale=-0.5), reads=[TQb], writes=[TGb])
                        pg.op("dve", lambda e, n=n, lc=lc: e.scalar_tensor_tensor(out=T1[:, 0:n], in0=OACC[:, lc:lc + n], scalar=hng2[:, 0:1], in1=RS[:, 0:n], op0=ALU.mult, op1=ALU.mult),
                              reads=OACCb[lc // TB:(lc + n) // TB] + [TGb, zb], writes=[TTb[0]])
                        pg.op("pool", lambda e, n=n, lc=lc, i=i: e.tensor_tensor(out=OGt[i][:, 0:n], in0=T1[:, 0:n], in1=GT[:, lc:lc + n], op=ALU.mult),
                              reads=[TTb[0], GTb], writes=[OGtb[i]])
                        pg.op("sp", lambda e, n=n, gc=gc, i=i, h=h: e.dma_start(out=OG[h * P:(h + 1) * P, gc:gc + n], in_=OGt[i][:, 0:n]),
                              reads=[OGtb[i]], writes=[OGb[h][gt]], chan=OGtb[i])
            pg.flush(es)

    def phase_conv1(l):
        jl = l // 2
        vertical = (jl % 2 == 1)
        with_ctx = (l == 1)
        with ExitStack() as es:
            W = [sb(es, f"cW{i}", [P, 2, KC, P], BF16) for i in range(2)]
            Wb = [pg.bufs_n(f"cW{i}_", 2) for i in range(2)]
            dg = [sb(es, f"dg{i}", [P, 31, P], BF16) for i in range(2)]
            dgb = pg.bufs_n("dg", 2)
            if vertical:
                up = [sb(es, f"up{b}", [P, 62, 64], BF16) for b in range(2)]
            else:
                up = [sb(es, f"up{b}", [P, 32, 94], BF16) for b in range(2)]
            upb = pg.bufs_n("up", 2)
            upc = sb(es, "upc", [P, 2, 286], BF16)
            upcb = pg.buf("upc")
            sg = [sb(es, f"sg{i}", [P, 512]) for i in range(2)]
            sgb = pg.bufs_n("sg", 2)
            uo = [sb(es, f"uo{i}", [P, 512]) for i in range(2)]
            uob = pg.bufs_n("uo", 2)
            pa = [ps(es, f"pa{i}", [P, 512]) for i in range(2)]
            pgl = [ps(es, f"pgl{i}", [P, 512]) for i in range(2)]
            pc = [ps(es, f"pc{i}", [P, 512]) for i in range(2)]
            pab, pglb, pcb = pg.bufs_n("pa", 2), pg.bufs_n("pgl", 2), pg.bufs_n("pc", 2)

            def init(e):
                for b in range(2):
                    e.memset(up[b][:].rearrange("p a b -> p (a b)"), 0.0)
                return e.memset(upc[:].rearrange("p a b -> p (a b)"), 0.0)
            pg.op("pool", init, writes=upb + [upcb])
            tcnt = 0
            ccnt = 0
            def load_chunk(c):
                wi = c % 2
                for s in range(2):
                    col0 = s * E + c * P
                    src = cwin_d[jl].rearrange("(kc p) n -> p kc n", p=P)[:, :, col0:col0 + P]
                    pg.op("pool", lambda e, s=s, wi=wi, src=src: e.dma_start(out=W[wi][:, s, :, :], in_=src), writes=[Wb[wi][s]], chan=Wb[wi][s])
                pg.op("pool", lambda e, c=c, wi=wi: e.tensor_tensor(out=dg[wi][:], in0=bc(identb[:].unsqueeze(1), [P, 31, P]),
                                                                    in1=bc(cdw[:, jl, c, :].unsqueeze(2), [P, 31, P]), op=ALU.mult),
                      reads=[CONSTb], writes=[dgb[wi]])
            load_chunk(0)
            load_chunk(1)
            for c in range(16):
                wi = c % 2
                if c >= 1 and c + 1 < 16:
                    load_chunk(c + 1)
                ba = cbin[:, jl, c:c + 1]
                bgl = cbin[:, jl, 16 + c:17 + c]
                segs = [(b, i) for b in range(2) for i in range(4)] + ([(2, 0)] if with_ctx else [])
                for (b, i) in segs:
                    k = tcnt % 2
                    tcnt += 1
                    if b < 2:
                        gc, gt = b * 2048 + i * 512, b * 4 + i
                    else:
                        gc, gt = NLAT, 8

                    def proj(e, s, dst, gc=gc, wi=wi):
                        ins = None
                        for kc in range(KC):
                            ins = e.matmul(dst[:], lhsT=W[wi][:, s, kc, :], rhs=hT[:, kc, gc:gc + 512], start=(kc == 0), stop=(kc == KC - 1))
                        return ins
                    pg.op("pe", lambda e, f=proj, k=k: f(e, 0, pa[k]), reads=[hTb[gt], Wb[wi][0]], writes=[pab[k]])
                    pg.op("pe", lambda e, f=proj, k=k: f(e, 1, pgl[k]), reads=[hTb[gt], Wb[wi][1]], writes=[pglb[k]])
                    pg.op("act", lambda e, k=k, bgl=bgl: e.activation(out=sg[k][:], in_=pgl[k][:], func=AF.Sigmoid, bias=bgl, scale=1.0),
                          reads=[pglb[k], CONSTb], writes=[sgb[k]])
                    if b < 2:
                        if vertical:
                            dst = up[b][:, 15 + i * 8:15 + i * 8 + 8, :]
                        else:
                            dst = up[b][:, i * 8:i * 8 + 8, 15:79]
                        pg.op("dve", lambda e, k=k, dst=dst, ba=ba: e.scalar_tensor_tensor(out=dst, in0=pa[k][:].rearrange("p (a b) -> p a b", b=64), scalar=ba,
                                                                                             in1=sg[k][:].rearrange("p (a b) -> p a b", b=64), op0=ALU.add, op1=ALU.mult),
                              reads=[pab[k], sgb[k], CONSTb], writes=[upb[b]])
                    else:
                        dst = upc[:, :, 15:271]
                        pg.op("dve", lambda e, k=k, dst=dst, ba=ba: e.scalar_tensor_tensor(out=dst, in0=pa[k][:].rearrange("p (a b) -> p a b", b=256), scalar=ba,
                                                                                             in1=sg[k][:].rearrange("p (a b) -> p a b", b=256), op0=ALU.add, op1=ALU.mult),
                              reads=[pab[k], sgb[k], CONSTb], writes=[upcb])
                for (b, i) in segs:
                    k = ccnt % 2
                    ccnt += 1
                    if b < 2:
                        gc, gt = b * 2048 + i * 512, b * 4 + i
                    else:
                        gc, gt = NLAT, 8

                    def conv(e, b=b, i=i, k=k, wi=wi):
                        taps = []
                        for j in range(31):
                            if b < 2 and vertical:
                                lo, hi = i * 8 + j - 15, i * 8 + 7 + j - 15
                                if hi < 0 or lo > 31:
                                    continue
                            taps.append(j)
                        ins = None
                        for n_, j in enumerate(taps):
                            if b == 2:
                                rhs = upc[:, :, j:j + 256]
                                out = pc[k][:].rearrange("p (a b) -> p a b", b=256)
                            elif vertical:
                                rhs = up[b][:, i * 8 + j:i * 8 + j + 8, :]
                                out = pc[k][:].rearrange("p (a b) -> p a b", b=64)
                            else:
                                rhs = up[b][:, i * 8:i * 8 + 8, j:j + 64]
                                out = pc[k][:].rearrange("p (a b) -> p a b", b=64)
                            ins = e.matmul(out, lhsT=dg[wi][:, j, :], rhs=rhs, start=(n_ == 0), stop=(n_ == len(taps) - 1))
                        return ins
                    pg.op("pe", conv, reads=[upb[b] if b < 2 else upcb, dgb[wi]], writes=[pcb[k]])
                    pg.op("act", lambda e, k=k, c=c: e.activation(out=uo[k][:], in_=pc[k][:], func=AF.Identity, bias=cdwb[:, jl, c:c + 1], scale=1.0),
                          reads=[pcb[k], CONSTb], writes=[uob[k]])
                    pg.op("sp", lambda e, k=k, c=c, gc=gc: e.dma_start(out=UC[c * P:(c + 1) * P, gc:gc + 512], in_=uo[k][:]),
                          reads=[uob[k]], writes=[UCb[c][gt]], chan=uob[k])
            pg.flush(es)

    def phase_conv2(l):
        jl = l // 2
        tiles = list(range(9)) if l == 1 else list(range(8))
        with ExitStack() as es:
            Wg = sb(es, "Wg", [P, KC, E], BF16)
            Wgb = pg.bufs_n("Wg", 4)
            uc = [sb(es, f"uct{i}", [P, EC, 256]) for i in range(2)]
            ucb = pg.bufs_n("uct", 2)
            ht = [sb(es, f"htt{i}", [P, KC, 256], BF16) for i in range(2)]
            htb = pg.bufs_n("htt", 2)
            sq = sb(es, "csq", [P, EC, 256])
            sqb = pg.buf("csq")
            mean = sb(es, "mean", [P, 256]); msq = sb(es, "msq", [P, 256]); var = sb(es, "var", [P, 256])
            lnv = sb(es, "clnv", [P, 256]); rstd = sb(es, "crstd", [P, 256]); mr = sb(es, "mr", [P, 256])
            stb = pg.buf("stats")
            xna = sb(es, "cxna", [P, EC, 256])
            s1a = xna
            s2a = sq
            xnab = pg.buf("cxna")
            s2ab = sqb
            s1ab = xnab
            ogt = [sb(es, "cog", [P, EC, 256], BF16)] * 2
            ogtb = [pg.buf("cog")] * 2
            p1 = ps(es, "p1", [P, 512]); p2 = ps(es, "p2", [P, 512])
            p1b, p2b = pg.buf("p1"), pg.buf("p2")
            pgp = [ps(es, f"pgp{i}", [P, 512]) for i in range(4)]
            pgpb = pg.bufs_n("pgp", 4)
            for q in range(4):
                src = cwin_d[jl].rearrange("(kc p) n -> p kc n", p=P)[:, q * 2:(q + 1) * 2, 2 * E:3 * E]
                pg.op("pool", lambda e, q=q, src=src: e.dma_start(out=Wg[:, q * 2:(q + 1) * 2, :], in_=src), writes=[Wgb[q]], chan=Wgb[q])
            n = 0
            gcnt = 0
            halves = [(t, hf) for t in tiles for hf in range(2)]

            def loads2(m):
                t, hf = halves[m]
                i = m % 2
                c0 = t * 512 + hf * 256
                pg.op("sp", lambda e, i=i, c0=c0: e.dma_start(out=uc[i][:], in_=UC.rearrange("(ec p) n -> p ec n", p=P)[:, :, c0:c0 + 256]),
                      reads=[UCb[c][t] for c in range(16)], writes=[ucb[i]], chan=ucb[i])
                pg.op("sp", lambda e, i=i, c0=c0: e.dma_start(out=ht[i][:], in_=HTd.rearrange("(kc p) n -> p kc n", p=P)[:, :, c0:c0 + 256]),
                      reads=[HTdb[t]], writes=[htb[i]], chan=htb[i])
            loads2(0)
            for t in tiles:
                for hf in range(2):
                    i = n % 2
                    n += 1
                    c0 = t * 512 + hf * 256
                    if n < len(halves):
                        loads2(n)
                    pg.op("act", lambda e, i=i: e.activation(out=sq[:], in_=uc[i][:], func=AF.Square), reads=[ucb[i]], writes=[sqb])

                    def st1(e, i=i):
                        ins = None
                        for ec in range(EC):
                            ins = e.matmul(p1[:, 0:256], lhsT=onesf[:], rhs=uc[i][:, ec, :], start=(ec == 0), stop=(ec == EC - 1))
                        return ins

                    def st2(e):
                        ins = None
                        for ec in range(EC):
                            ins = e.matmul(p2[:, 0:256], lhsT=onesf[:], rhs=sq[:, ec, :], start=(ec == 0), stop=(ec == EC - 1))
                        return ins
                    pg.op("pe", st1, reads=[ucb[i], CONSTb], writes=[p1b])
                    pg.op("pe", st2, reads=[sqb, CONSTb], writes=[p2b])

                    pg.op("dve", lambda e: e.tensor_scalar(out=mean[:], in0=p1[:, 0:256], scalar1=1.0 / E, scalar2=None, op0=ALU.mult), reads=[p1b], writes=[stb])
                    pg.op("dve", lambda e: e.tensor_tensor(out=msq[:], in0=mean[:], in1=mean[:], op=ALU.mult), reads=[stb], writes=[stb])
                    pg.op("dve", lambda e: e.scalar_tensor_tensor(out=var[:], in0=p2[:, 0:256], scalar=1.0 / E, in1=msq[:], op0=ALU.mult, op1=ALU.subtract),
                          reads=[stb, p2b], writes=[stb])
                    pg.op("act", lambda e: e.activation(out=lnv[:], in_=var[:], func=AF.Ln, bias=EPS, scale=1.0), reads=[stb], writes=[stb])
                    pg.op("act", lambda e: e.activation(out=rstd[:], in_=lnv[:], func=AF.Exp, scale=-0.5), reads=[stb], writes=[stb])
                    pg.op("dve", lambda e: e.tensor_tensor(out=mr[:], in0=mean[:], in1=rstd[:], op=ALU.mult), reads=[stb], writes=[stb])
                    pg.op("dve", lambda e, i=i: e.tensor_tensor(out=xna[:], in0=uc[i][:], in1=bc(rstd[:].unsqueeze(1), [P, EC, 256]), op=ALU.mult),
                          reads=[ucb[i], stb], writes=[xnab])
                    pg.op("pool", lambda e: e.tensor_tensor(out=xna[:], in0=xna[:], in1=bc(mr[:].unsqueeze(1), [P, EC, 256]), op=ALU.subtract),
                          reads=[xnab, stb], writes=[xnab])
                    for ec in range(EC):
                        k = gcnt % 4
                        gcnt += 1

                        def gp(e, i=i, ec=ec, k=k):
                            ins = None
                            for kc in range(KC):
                                ins = e.matmul(pgp[k][:, 0:256], lhsT=Wg[:, kc, ec * P:(ec + 1) * P], rhs=ht[i][:, kc, :], start=(kc == 0), stop=(kc == KC - 1))
                            return ins
                        pg.op("pe", gp, reads=[htb[i]] + Wgb, writes=[pgpb[k]])
                        pg.op("act", lambda e, k=k, ec=ec: e.activation(out=s2a[:, ec, :], in_=pgp[k][:, 0:256], func=AF.Silu, bias=cbin[:, jl, 32 + ec:33 + ec], scale=1.0),
                              reads=[pgpb[k], CONSTb], writes=[s2ab])
                    for ec in range(EC):
                        pg.op("act", lambda e, ec=ec: e.activation(out=s1a[:, ec, :], in_=xna[:, ec, :], func=AF.Silu, scale=clng[:, jl, ec:ec + 1], bias=clnb[:, jl, ec:ec + 1]),
                              reads=[xnab, CONSTb], writes=[s1ab])
                    pg.op("dve", lambda e, i=i: e.tensor_tensor(out=ogt[i][:], in0=s1a[:], in1=s2a[:], op=ALU.mult),
                          reads=[s1ab, s2ab], writes=[ogtb[i]])
                    pg.op("sp", lambda e, i=i, c0=c0: e.dma_start(out=OG.rearrange("(ec p) n -> p ec n", p=P)[:, :, c0:c0 + 256], in_=ogt[i][:]),
                          reads=[ogtb[i]], writes=[OGb[hf][t]], chan=ogtb[i])
            pg.flush(es)

    for l in range(n_layers):
        tiles_in = list(range(9)) if l < 3 else list(range(8))
        tiles_out = list(range(9)) if l < 2 else list(range(8))
        phase_norm(l, l, tiles_in)
        if l % 2 == 0:
            if "hgrn" not in SKIP:
                phase_hgrn(l)
                phase_out(l, hwout_d[l // 2], OGb, tiles_out)
        else:
            if "conv1" not in SKIP:
                phase_conv1(l)
            if "conv2" not in SKIP:
                phase_conv2(l)
            if "cout" not in SKIP:
                phase_out(l, cwout_d[l // 2], OGb[0:2], tiles_out)
    if dbg:
        pass
    phase_norm(0, n_layers, list(range(8)), final=True)
    top.close()
    return nc


_CACHE = {}


def _prep_inputs(inp):
    f = lambda a: np.ascontiguousarray(np.asarray(a, dtype=np.float32))
    x = f(inp["x"]); c = f(inp["c"]); ctx = f(inp["ctx"]); c_ctx = f(inp["c_ctx"])

    def colT(v, nch):
        return np.ascontiguousarray(v.reshape(nch, P).T)
    shared = {
        "normgT": np.ascontiguousarray(np.stack([colT(f(inp["norm_g"])[l], KC) for l in range(4)], axis=1).reshape(P, 4 * KC)),
        "fingT": colT(f(inp["final_norm_g"]), KC),
        "ada_w": f(inp["ada_w"]),
        "adabT": np.ascontiguousarray(np.stack([colT(f(inp["ada_b"])[l], 24) for l in range(4)], axis=1).reshape(P, 96)),
        "hgrn_w_in": f(inp["hgrn_w_in"]),
        "hgrn_w_out": f(inp["hgrn_w_out"]),
        "lbT": np.ascontiguousarray(f(inp["hgrn_lb_logits"]).reshape(2, 2, 16, P).transpose(3, 0, 1, 2).reshape(P, 64)),
        "hngT": np.ascontiguousarray(f(inp["hgrn_norm_g"]).T),
        "conv_w_in": f(inp["conv_w_in"]),
        "conv_w_out": f(inp["conv_w_out"]),
        "cbinT": np.ascontiguousarray(np.stack([colT(f(inp["conv_b_in"])[j], 48) for j in range(2)], axis=1).reshape(P, 96)),
        "cdwT": np.ascontiguousarray(f(inp["conv_dw"]).reshape(2, 31, 16, P).transpose(3, 0, 2, 1).reshape(P, 2 * 16 * 31)),
        "cdwbT": np.ascontiguousarray(np.stack([colT(f(inp["conv_dw_b"])[j], 16) for j in range(2)], axis=1).reshape(P, 32)),
        "clngT": np.ascontiguousarray(np.stack([colT(f(inp["conv_ln_g"])[j], 16) for j in range(2)], axis=1).reshape(P, 32)),
        "clnbT": np.ascontiguousarray(np.stack([colT(f(inp["conv_ln_b"])[j], 16) for j in range(2)], axis=1).reshape(P, 32)),
        "cboutT": np.ascontiguousarray(np.stack([colT(f(inp["conv_b_out"])[j], 8) for j in range(2)], axis=1).reshape(P, 16)),
        "ident": np.eye(P, dtype=np.float32),
    }
    s = np.arange(P)[:, None]
    t = np.arange(P)[None, :]
    shared["masks"] = np.ascontiguousarray(np.concatenate([(s <= t), (s >= t)], axis=1).astype(np.float32))
    r = np.ones((P, 2, 512), np.float32)
    r[:, 0, 0::64] = 0.0
    r[:, 1, 63::64] = 0.0
    shared["resets"] = np.ascontiguousarray(r.reshape(P, 1024))
    maps = []
    for k in range(8):
        b0, b1 = 2 * k, 2 * k + 1
        xT = np.empty((D, NT), np.float32)
        xT[:, 0:2048] = x[b0].T
        xT[:, 2048:4096] = x[b1].T
        xT[:, 4096:4352] = ctx[b0].T
        xT[:, 4352:4608] = ctx[b1].T
        cv = np.stack([c[b0], c[b1], c_ctx], axis=1)
        cT = np.ascontiguousarray(cv.reshape(KC, P, 3).transpose(1, 0, 2).reshape(P, KC * 3))
        m = dict(shared)
        m["xT0"] = xT
        m["cT"] = cT
        maps.append(m)
    return maps


def kernel(**inputs):
    if "nc" not in _CACHE:
        _CACHE["nc"] = build()
    nc = _CACHE["nc"]
    maps = _prep_inputs(inputs)
    res = run_bass_kernel_spmd(nc, maps, core_ids=list(range(8)))
    out = np.empty((16, 2048, D), np.float32)
    for k in range(8):
        oT = np.asarray(res.results[k]["outT"])
        out[2 * k] = oT[:, 0:2048].T
        out[2 * k + 1] = oT[:, 2048:4096].T
    return out
```
